# Optimizing a Trainium2 kernel written in Bass

```python
import jax, jax.numpy as jnp
from jax import lax
import numpy as np

D_MODEL = 4096
BATCH = 4
SEQ = 4096
DEPTH = 1

N_META = 16
HEAD_DIM = 64
N_Q_HEADS = 32
N_KV_HEADS = 4
GROUP = N_Q_HEADS // N_KV_HEADS
WINDOW = 128
BLOCK = 128
ROPE_THETA = 10000.0
Q_DIM = N_Q_HEADS * HEAD_DIM
KV_DIM = N_KV_HEADS * HEAD_DIM
CONV_DIM = D_MODEL // 2
CONV_WIDTH = 31
FFN_DIM = ((8 * D_MODEL + 3 * 256 - 1) // (3 * 256)) * 256
IN_DIM = Q_DIM + 2 * KV_DIM + 2 * CONV_DIM + 2 * D_MODEL
EPS = 1e-6

kernel_name = "hybrid_swa_sink_conformer_conv_gated_block"


def rms_norm(x, g):
    xf = x.astype(jnp.float32)
    y = xf * lax.rsqrt(jnp.mean(xf * xf, axis=-1, keepdims=True) + EPS)
    return (y * g.astype(jnp.float32)).astype(x.dtype)


def layer_norm(x, g, b):
    xf = x.astype(jnp.float32)
    mu = jnp.mean(xf, axis=-1, keepdims=True)
    xc = xf - mu
    y = xc * lax.rsqrt(jnp.mean(xc * xc, axis=-1, keepdims=True) + EPS)
    return (y * g.astype(jnp.float32) + b.astype(jnp.float32)).astype(x.dtype)


def rope_tables(length):
    pos = jnp.arange(length, dtype=jnp.float32)
    inv_freq = ROPE_THETA ** (-jnp.arange(0, HEAD_DIM, 2, dtype=jnp.float32) / HEAD_DIM)
    ang = pos[:, None] * inv_freq[None, :]
    return jnp.cos(ang), jnp.sin(ang)


def apply_rope(x, cos, sin):
    xf = x.astype(jnp.float32)
    x1, x2 = jnp.split(xf, 2, axis=-1)
    c = cos[None, :, None, :]
    s = sin[None, :, None, :]
    return jnp.concatenate([x1 * c - x2 * s, x2 * c + x1 * s], axis=-1).astype(x.dtype)


def sliding_window_attention(q, k, v, sinks):
    b, length = q.shape[0], q.shape[1]
    pad = BLOCK - N_META
    padded = length + pad
    nb = padded // BLOCK
    padw = ((0, 0), (pad, 0), (0, 0), (0, 0))
    qb = jnp.pad(q, padw).reshape(b, nb, BLOCK, N_KV_HEADS, GROUP, HEAD_DIM)
    kb = jnp.pad(k, padw).reshape(b, nb, BLOCK, N_KV_HEADS, HEAD_DIM)
    vb = jnp.pad(v, padw).reshape(b, nb, BLOCK, N_KV_HEADS, HEAD_DIM)
    k_prev = jnp.concatenate([jnp.zeros_like(kb[:, :1]), kb[:, :-1]], axis=1)
    v_prev = jnp.concatenate([jnp.zeros_like(vb[:, :1]), vb[:, :-1]], axis=1)
    k_band = jnp.concatenate([k_prev, kb], axis=2)
    v_band = jnp.concatenate([v_prev, vb], axis=2)
    k_meta = k[:, :N_META]
    v_meta = v[:, :N_META]
    scale = HEAD_DIM ** -0.5

    s_band = jnp.einsum('bnqhgd,bnkhd->bnhgqk', qb, k_band).astype(jnp.float32) * scale
    s_meta = jnp.einsum('bnqhgd,bmhd->bnhgqm', qb, k_meta).astype(jnp.float32) * scale

    blk = jnp.arange(nb)[:, None]
    q_pos = blk * BLOCK + jnp.arange(BLOCK)[None, :] - pad
    k_pos = (blk - 1) * BLOCK + jnp.arange(2 * BLOCK)[None, :] - pad
    qp = q_pos[:, :, None]
    kp = k_pos[:, None, :]
    band_mask = (kp >= N_META) & (kp <= qp) & (qp - kp < WINDOW)
    meta_mask = jnp.arange(N_META)[None, None, :] <= qp

    neg = jnp.float32(-jnp.inf)
    s_band = jnp.where(band_mask[None, :, None, None], s_band, neg)
    s_meta = jnp.where(meta_mask[None, :, None, None], s_meta, neg)
    sink = jnp.broadcast_to(
        sinks.astype(jnp.float32).reshape(N_KV_HEADS, GROUP)[None, None, :, :, None, None],
        s_band.shape[:-1] + (1,))
    probs = jax.nn.softmax(jnp.concatenate([s_band, s_meta, sink], axis=-1), axis=-1)
    p_band = probs[..., :2 * BLOCK].astype(v.dtype)
    p_meta = probs[..., 2 * BLOCK:2 * BLOCK + N_META].astype(v.dtype)
    o = (jnp.einsum('bnhgqk,bnkhd->bnqhgd', p_band, v_band)
         + jnp.einsum('bnhgqm,bmhd->bnqhgd', p_meta, v_meta))
    return o.reshape(b, padded, Q_DIM)[:, pad:]


def conformer_conv(c_in, conv_w, conv_b, ln_g, ln_b, w_co, b_co):
    a, g = jnp.split(c_in, 2, axis=-1)
    c = a * jax.nn.sigmoid(g)
    c = lax.conv_general_dilated(
        c, conv_w.astype(c.dtype), window_strides=(1,), padding=[(CONV_WIDTH - 1, 0)],
        dimension_numbers=('NWC', 'WIO', 'NWC'), feature_group_count=CONV_DIM) + conv_b
    c = layer_norm(c, ln_g, ln_b)
    c = c * jax.nn.sigmoid(c)
    return c @ w_co + b_co


def mixer_block(u, cos, sin, w_in, b_in, sinks, conv_w, conv_b, ln_g, ln_b,
                w_ao, w_co, b_co, w_out):
    b, length, _ = u.shape
    z = u @ w_in + b_in
    idx = np.cumsum([Q_DIM, KV_DIM, KV_DIM, 2 * CONV_DIM, D_MODEL])
    q, k, v, c_in, gate_a, gate_b = jnp.split(z, idx, axis=-1)
    q = apply_rope(q.reshape(b, length, N_Q_HEADS, HEAD_DIM), cos, sin)
    k = apply_rope(k.reshape(b, length, N_KV_HEADS, HEAD_DIM), cos, sin)
    v = v.reshape(b, length, N_KV_HEADS, HEAD_DIM)
    branch_a = sliding_window_attention(q, k, v, sinks) @ w_ao
    branch_b = conformer_conv(c_in, conv_w, conv_b, ln_g, ln_b, w_co, b_co)
    merged = jax.nn.sigmoid(gate_a) * branch_a + jax.nn.sigmoid(gate_b) * branch_b
    return merged @ w_out


def swiglu(u, w_gate_up, w_down):
    gu = u @ w_gate_up
    g, up = jnp.split(gu, 2, axis=-1)
    return (jax.nn.silu(g) * up) @ w_down


def setup_inputs(seed: int = 0) -> dict:
    key = jax.random.key(seed)
    ks = jax.random.split(key, 20)
    f32 = jnp.float32
    nrm = lambda k, shape, s: jax.random.normal(k, shape, f32) * s
    return {
        "x": nrm(ks[0], (BATCH, SEQ, D_MODEL), 1.0),
        "meta_tokens": nrm(ks[1], (N_META, D_MODEL), 1.0),
        "mix_norm_g": 1.0 + nrm(ks[2], (DEPTH, D_MODEL), 0.02),
        "w_in": nrm(ks[3], (DEPTH, D_MODEL, IN_DIM), D_MODEL ** -0.5),
        "b_in": nrm(ks[4], (DEPTH, IN_DIM), 0.02),
        "attn_sinks": nrm(ks[5], (DEPTH, N_Q_HEADS), 1.0),
        "conv_w": nrm(ks[6], (DEPTH, CONV_WIDTH, 1, CONV_DIM), CONV_WIDTH ** -0.5),
        "conv_b": nrm(ks[7], (DEPTH, CONV_DIM), 0.02),
        "conv_ln_g": 1.0 + nrm(ks[8], (DEPTH, CONV_DIM), 0.02),
        "conv_ln_b": nrm(ks[9], (DEPTH, CONV_DIM), 0.02),
        "w_attn_o": nrm(ks[10], (DEPTH, Q_DIM, D_MODEL), Q_DIM ** -0.5),
        "w_conv_o": nrm(ks[11], (DEPTH, CONV_DIM, D_MODEL), CONV_DIM ** -0.5),
        "b_conv_o": nrm(ks[12], (DEPTH, D_MODEL), 0.02),
        "w_out": nrm(ks[13], (DEPTH, D_MODEL, D_MODEL), D_MODEL ** -0.5),
        "ffn_norm_g": 1.0 + nrm(ks[14], (DEPTH, D_MODEL), 0.02),
        "w_gate_up": nrm(ks[15], (DEPTH, D_MODEL, 2 * FFN_DIM), D_MODEL ** -0.5),
        "w_down": nrm(ks[16], (DEPTH, FFN_DIM, D_MODEL), FFN_DIM ** -0.5),
        "final_norm_g": 1.0 + nrm(ks[17], (D_MODEL,), 0.02),
    }


def reference(x, meta_tokens, mix_norm_g, w_in, b_in, attn_sinks, conv_w, conv_b,
              conv_ln_g, conv_ln_b, w_attn_o, w_conv_o, b_conv_o, w_out,
              ffn_norm_g, w_gate_up, w_down, final_norm_g):
    b = x.shape[0]
    meta = jnp.broadcast_to(meta_tokens[None].astype(x.dtype), (b, N_META, D_MODEL))
    h = jnp.concatenate([meta, x], axis=1)
    cos, sin = rope_tables(h.shape[1])
    for layer in range(DEPTH):
        u = rms_norm(h, mix_norm_g[layer])
        h = h + mixer_block(u, cos, sin, w_in[layer], b_in[layer], attn_sinks[layer],
                            conv_w[layer], conv_b[layer], conv_ln_g[layer], conv_ln_b[layer],
                            w_attn_o[layer], w_conv_o[layer], b_conv_o[layer], w_out[layer])
        h = h + swiglu(rms_norm(h, ffn_norm_g[layer]), w_gate_up[layer], w_down[layer])
    y = rms_norm(h, final_norm_g)
    return y[:, N_META:]
```

```python
import numpy as np
import concourse.bass as bass
import concourse.mybir as mybir
from concourse.bass_utils import run_bass_kernel_spmd

F32 = mybir.dt.float32
BF16 = mybir.dt.bfloat16
AF = mybir.ActivationFunctionType
ALU = mybir.AluOpType

D = 4096
SEQ = 4096
BATCH = 4
N_META = 16
HD = 64
NQH = 32
NKV = 4
QD = 2048
KVD = 256
CD = 2048
CW = 31
FFN = 11008
IN_DIM = QD + 2 * KVD + 2 * CD + 2 * D
EPS = 1e-6
T = 512
NG = 4
HALO = 128
TS = HALO + T + N_META
KC = D // 128
NCC = CD // 128
NFT = FFN // 128
FG_SIZES = [11, 11, 11, 11, 11, 11, 10, 10]
NCORES = 8

SAME_ENGINE_SYNC = True

C_BIN = 0
C_BCO = C_BIN + 116
C_G1 = C_BCO + 32
C_G2 = C_G1 + 32
C_GF = C_G2 + 32
C_CW = C_GF + 32
C_CB = C_CW + NCC * CW
C_LG = C_CB + NCC
C_LB = C_LG + NCC
C_SK = C_LB + NCC
C_PERM = C_SK + 16
C_ID = C_PERM + 128
NCST = C_ID + 128


def in_units():
    units = []
    for jt in range(16):
        pair, j = jt // 8, jt % 8
        hlo = (2 * pair) * 8 + j
        hhi = (2 * pair + 1) * 8 + j
        cols = np.concatenate([hlo * 64 + np.arange(64), hhi * 64 + np.arange(64)])
        units.append(("q%d" % jt, cols))
    for kp in range(2):
        units.append(("k%d" % kp, QD + kp * 128 + np.arange(128)))
    for u in range(2):
        units.append(("v%d" % u, QD + KVD + u * 128 + np.arange(128)))
    c0 = QD + 2 * KVD
    for cc in range(NCC):
        units.append(("ca%d" % cc, c0 + cc * 128 + np.arange(128)))
        units.append(("cg%d" % cc, c0 + CD + cc * 128 + np.arange(128)))
    g0 = c0 + 2 * CD
    for j in range(KC):
        units.append(("ga%d" % j, g0 + j * 128 + np.arange(128)))
        units.append(("gb%d" % j, g0 + D + j * 128 + np.arange(128)))
    return units


def ot_rows():
    rows = []
    for hk in range(NKV):
        for c in range(4):
            hlo = 8 * hk + c
            hhi = 8 * hk + 4 + c
            rows.append(hlo * 64 + np.arange(64))
            rows.append(hhi * 64 + np.arange(64))
    return np.concatenate(rows)


def unit_schedule():
    sched = []
    iu = in_units()
    for name, _ in iu[:16 + 2 + 2 + 2 * NCC]:
        sched.append((name, 4096))
    for j in range(KC):
        sched.append(("ga%d" % j, 4096))
        sched.append(("gb%d" % j, 4096))
        sched.append(("ao%d" % j, 2048))
        sched.append(("co%d" % j, 2048))
    for j in range(KC):
        sched.append(("wo%d" % j, 4096))
    f0 = 0
    for fg, n in enumerate(FG_SIZES):
        for ft in range(n):
            sched.append(("fgate%d" % (f0 + ft), 4096))
            sched.append(("fup%d" % (f0 + ft), 4096))
        for j in range(KC):
            sched.append(("wd%d_%d" % (fg, j), n * 128))
        f0 += n
    return sched


def _tile_units(w, kchunks):
    K, N = w.shape
    return np.ascontiguousarray(
        w.reshape(kchunks, 128, N // 128, 128).transpose(2, 1, 0, 3)).reshape(N // 128, 128, kchunks * 128)


def pack_weights(w_in, w_ao, w_co, w_out, w_gu, w_down):
    sched = unit_schedule()
    tot = sum(n for _, n in sched)
    wts = np.empty((128, tot), np.float32)
    iu = in_units()
    allcols = np.concatenate([c for _, c in iu])
    win_p = _tile_units(np.ascontiguousarray(w_in[:, allcols]), KC)
    in_idx = {name: i for i, (name, _) in enumerate(iu)}
    ao_u = _tile_units(np.ascontiguousarray(w_ao[ot_rows(), :]), 16)
    co_u = _tile_units(w_co, 16)
    wo_u = _tile_units(w_out, KC)
    gu_u = _tile_units(w_gu, KC)
    off = 0
    f0s = np.cumsum([0] + FG_SIZES)
    for name, n in sched:
        if name in in_idx:
            blk = win_p[in_idx[name]]
        elif name.startswith("ao"):
            blk = ao_u[int(name[2:])]
        elif name.startswith("co"):
            blk = co_u[int(name[2:])]
        elif name.startswith("wo"):
            blk = wo_u[int(name[2:])]
        elif name.startswith("fgate"):
            blk = gu_u[int(name[5:])]
        elif name.startswith("fup"):
            blk = gu_u[NFT + int(name[3:])]
        elif name.startswith("wd"):
            fg, j = name[2:].split("_")
            fg, j = int(fg), int(j)
            f0, nf = f0s[fg], FG_SIZES[fg]
            sub = w_down[f0 * 128:(f0 + nf) * 128, j * 128:(j + 1) * 128]
            blk = sub.reshape(nf, 128, 128).transpose(1, 0, 2).reshape(128, nf * 128)
        else:
            raise KeyError(name)
        wts[:, off:off + n] = blk
        off += n
    assert off == tot
    return wts


def pack_consts(b_in, b_co, g1, g2, gf, conv_w, conv_b, ln_g, ln_b, sinks, b_v_dummy=None):
    cst = np.zeros((128, NCST), np.float32)
    iu = in_units()
    for i, (name, cols) in enumerate(iu):
        cst[:, C_BIN + i] = b_in[cols]
    cst[:, C_BCO:C_BCO + 32] = b_co.reshape(32, 128).T
    cst[:, C_G1:C_G1 + 32] = g1.reshape(32, 128).T
    cst[:, C_G2:C_G2 + 32] = g2.reshape(32, 128).T
    cst[:, C_GF:C_GF + 32] = gf.reshape(32, 128).T
    cw = conv_w.reshape(CW, CD)
    cst[:, C_CW:C_CW + NCC * CW] = cw.reshape(CW, NCC, 128).transpose(2, 1, 0).reshape(128, NCC * CW)
    cst[:, C_CB:C_CB + NCC] = conv_b.reshape(NCC, 128).T
    cst[:, C_LG:C_LG + NCC] = ln_g.reshape(NCC, 128).T
    cst[:, C_LB:C_LB + NCC] = ln_b.reshape(NCC, 128).T
    for hk in range(NKV):
        for c in range(4):
            cst[0:64, C_SK + hk * 4 + c] = sinks[8 * hk + c]
            cst[64:128, C_SK + hk * 4 + c] = sinks[8 * hk + 4 + c]
    perm = np.zeros((128, 128), np.float32)
    for m in range(128):
        partner = m + 32 if (m % 64) < 32 else m - 32
        perm[partner, m] = 1.0
    cst[:, C_PERM:C_PERM + 128] = perm
    cst[:, C_ID:C_ID + 128] = np.eye(128, dtype=np.float32)
    return cst


def rope_tables(pos):
    inv_freq = (np.float32(10000.0) ** (-np.arange(0, HD, 2, dtype=np.float32) / np.float32(HD))).astype(np.float32)
    ang = (pos[:, None].astype(np.float32) * inv_freq[None, :]).astype(np.float32)
    cos = np.cos(ang).astype(np.float32)
    sin = np.sin(ang).astype(np.float32)
    m = np.arange(128)
    f = m % 32
    sign = np.where((m % 64) < 32, -1.0, 1.0).astype(np.float32)
    cosT = np.ascontiguousarray(cos[:, f].T)
    sinT = np.ascontiguousarray(sin[:, f].T * sign[:, None])
    return cosT, sinT


def core_inputs(core, x, meta):
    b, half = core // 2, core % 2
    xT = np.empty((NG, 128, KC, TS), np.float32)
    tabs = np.empty((NG, 128, 2, TS), np.float32)
    masks = np.empty((NG, 128, 3 * 512), np.float32)
    cmask = np.ones((NG, 128, 128), np.float32)
    jj = np.arange(128)[:, None]
    ii = np.arange(128)[None, :]
    mprev = (jj > ii).astype(np.float32)
    mcur = (jj <= ii).astype(np.float32)
    for g in range(NG):
        s0 = half * 2048 + g * T
        slab = np.zeros((TS, D), np.float32)
        pos = np.zeros((TS,), np.float32)
        if s0 >= HALO:
            slab[0:HALO] = x[b, s0 - HALO:s0]
            pos[0:HALO] = N_META + np.arange(s0 - HALO, s0)
            m0 = mprev
        else:
            slab[HALO - N_META:HALO] = meta
            pos[HALO - N_META:HALO] = np.arange(N_META)
            m0 = np.zeros_like(mprev)
            cmask[g, :, 0:HALO - N_META] = 0.0
        slab[HALO:HALO + T] = x[b, s0:s0 + T]
        pos[HALO:HALO + T] = N_META + np.arange(s0, s0 + T)
        slab[HALO + T:] = meta
        pos[HALO + T:] = np.arange(N_META)
        xT[g] = slab.T.reshape(KC, 128, TS).transpose(1, 0, 2)
        c, s = rope_tables(pos)
        tabs[g, :, 0, :] = c
        tabs[g, :, 1, :] = s
        masks[g, :, 0:512] = np.tile(m0, (1, 4))
        masks[g, :, 512:1024] = np.tile(mprev, (1, 4))
        masks[g, :, 1024:1536] = np.tile(mcur, (1, 4))
    return {"xT": xT, "tabs": tabs, "masks": masks, "cmask": cmask}


class Tok:
    __slots__ = ("sem", "val", "eng", "key")

    def __init__(self, sem, val, eng, key):
        self.sem, self.val, self.eng, self.key = sem, val, eng, key


class Eng:
    def __init__(self, name, sem, is_pe=False):
        self.name, self.sem, self.is_pe = name, sem, is_pe
        self.n = 0
        self.known = {}
        self.prog = []

    def record(self, fn, deps, inc=True):
        waits = []
        for d in deps:
            if d is None:
                continue
            if self.known.get(d.key, 0) >= d.val:
                continue
            self.known[d.key] = d.val
            waits.append((d.sem, d.val))
        tok = None
        if inc:
            self.n += 1
            tok = Tok(self.sem, self.n, self, "E" + self.name)
        self.prog.append((waits, fn, inc))
        return tok

    def replay(self, h):
        for waits, fn, inc in self.prog:
            for s, v in waits:
                h.wait_ge(s, v)
            if fn is None:
                continue
            ins = fn(h)
            if inc:
                ins.then_inc(self.sem, 1)


class Res:
    __slots__ = ("w", "r")

    def __init__(self):
        self.w = None
        self.r = {}


def res_tokens(rs):
    out = []
    for r in rs:
        if r.w is not None:
            out.append(r.w)
        out.extend(r.r.values())
    return out


def op(eng, fn, reads=(), writes=(), extra=(), inc=True, tok_override=None):
    deps = []
    for r in reads:
        if r.w is not None:
            deps.append(("raw", r.w))
    for w in writes:
        if w.w is not None:
            deps.append(("waw", w.w))
        for t in w.r.values():
            deps.append(("war", t))
    for t in extra:
        deps.append(("raw", t))
    fl = []
    for kind, t in deps:
        if t.eng is eng:
            if eng.is_pe:
                continue
            if not (kind == "raw" and SAME_ENGINE_SYNC):
                continue
        fl.append(t)
    tok = eng.record(fn, fl, inc)
    if tok_override is not None:
        tok = tok_override
    if tok is not None:
        for r in reads:
            r.r[tok.key] = tok
        for w in writes:
            w.w = tok
            w.r = {}
    return tok


class DmaSlot:
    def __init__(self, sem, name):
        self.sem, self.name, self.cnt = sem, name, 0
        self.res = Res()

    def next_tok(self):
        self.cnt += 1
        return Tok(self.sem, 16 * self.cnt, None, "D" + self.name)


def dma(queue, slot, out_ap, in_ap, reads=(), writes=(), extra=()):
    tok = slot.next_tok()
    sem = slot.sem

    def fn(h):
        return h.dma_start(out=out_ap, in_=in_ap).then_inc(sem, 16)

    op(queue, fn, reads=reads, writes=writes, extra=extra, inc=False, tok_override=tok)
    return tok


def build_nc(ng=NG, debug=False):
    nc = bass.Bass("TRN2", target_bir_lowering=False)
    sched = unit_schedule()
    tot_cols = sum(n for _, n in sched)
    xT_d = nc.dram_tensor("xT", [ng, 128, KC, TS], F32, kind="ExternalInput").ap()
    tabs_d = nc.dram_tensor("tabs", [ng, 128, 2, TS], F32, kind="ExternalInput").ap()
    masks_d = nc.dram_tensor("masks", [ng, 128, 1536], F32, kind="ExternalInput").ap()
    cmask_d = nc.dram_tensor("cmask", [ng, 128, 128], F32, kind="ExternalInput").ap()
    cst_d = nc.dram_tensor("cst", [128, NCST], F32, kind="ExternalInput").ap()
    bv_d = nc.dram_tensor("bv", [128, 256], F32, kind="ExternalInput").ap()
    wts_d = nc.dram_tensor("wts", [128, tot_cols], F32, kind="ExternalInput").ap()
    out_d = nc.dram_tensor("outT", [ng, D, T], F32, kind="ExternalOutput").ap()
    dbg = {}
    if debug:
        for nm, shp, dt_ in [("d_u", [128, KC * TS], BF16), ("d_q", [128, 16 * T], BF16), ("d_k", [128, 2 * TS], BF16),
                             ("d_ot", [128, 16 * T], BF16), ("d_cs", [128, 16 * T], BF16), ("d_mg", [128, 32 * T], BF16),
                             ("d_h1", [128, 32 * T], F32), ("d_y", [128, 16 * T], F32)]:
            dbg[nm] = nc.dram_tensor(nm, shp, dt_, kind="ExternalOutput").ap()

    from contextlib import ExitStack
    with ExitStack() as es:
        def sb(name, shape, dt):
            return es.enter_context(nc.sbuf_tensor(name, shape, dt))

        def sem(name):
            return es.enter_context(nc.semaphore(name))

        R1 = sb("R1", [128, 16384], F32)
        R2 = sb("R2", [128, 8192], F32)
        U = sb("U", [128, KC, TS], BF16)
        WS = [sb("W%d" % i, [128, 4096], BF16) for i in range(4)]
        XR = [sb("XR%d" % i, [128, TS], F32) for i in range(3)]
        SQ = [sb("SQ%d" % i, [128, TS], BF16) for i in range(2)]
        RSTD = sb("RSTD", [128, TS], F32)
        ST = [sb("ST%d" % i, [128, T], F32) for i in range(3)]
        MSK = sb("MSK", [128, 1536], BF16)
        CMK = sb("CMK", [128, 128], F32)
        CST = sb("CST", [128, NCST], F32)
        BV = sb("BV", [128, 256], F32)
        ONEB = sb("ONEB", [128, 128], BF16)
        ONEF = sb("ONEF", [128, 128], F32)
        ONELH = sb("ONELH", [128, 2, 128], BF16)
        ESK = sb("ESK", [128, 16], F32)
        IDB = sb("IDB", [128, 128], BF16)
        FT = [sb("FT%d" % i, [128, T], F32) for i in range(4)]
        banks = [es.enter_context(nc.psum_tensor("PS%d" % i, [128, 512], F32)) for i in range(8)]

        H1 = R1[:, :].rearrange("p (a b) -> p a b", b=T)
        QT = R1[:, 0:4096].bitcast(BF16).rearrange("p (a b) -> p a b", b=T)
        CSB = QT
        OT = R1[:, 4096:8192].bitcast(BF16).rearrange("p (a b) -> p a b", b=T)
        KT = R1[:, 8192:8192 + TS].bitcast(BF16).rearrange("p (a b) -> p a b", b=TS)
        VA = R1[:, 8848:8848 + 3072].bitcast(BF16).rearrange("p (t h v c) -> p t h v c", t=6, h=4, v=2)
        VA_flat = R1[:, 8848:8848 + 3072].bitcast(BF16)
        CB = [R1[:, 11920 + i * 320:11920 + (i + 1) * 320].bitcast(BF16) for i in range(2)]
        NDG = 6
        DG = [R1[:, 15792 + i * 64:15792 + (i + 1) * 64].bitcast(BF16) for i in range(NDG)]
        SG = [R1[:, 13200 + i * 640:13200 + (i + 1) * 640] for i in range(2)]
        QF = [R1[:, 14480 + i * TS:14480 + (i + 1) * TS] for i in range(2)]
        TAB = R2[:, 0:2 * TS].rearrange("p (a b) -> p a b", b=TS)
        RT = [R2[:, 1312 + i * TS:1312 + (i + 1) * TS] for i in range(2)]
        ET = [R2[:, 4096 + i * 256:4096 + (i + 1) * 256].bitcast(BF16) for i in range(4)]
        PT = [R2[:, 5120 + i * 256:5120 + (i + 1) * 256].bitcast(BF16) for i in range(8)]
        PM = [R2[:, 7168 + i * 256:7168 + (i + 1) * 256].bitcast(BF16) for i in range(4)]
        Y = R2[:, :].rearrange("p (a b) -> p a b", b=T)
        MG = R2[:, :].bitcast(BF16).rearrange("p (a b) -> p a b", b=T)
        ACT = [R2[:, i * 2816:(i + 1) * 2816].bitcast(BF16).rearrange("p (a b) -> p a b", b=T) for i in range(2)]

        PE = Eng("pe", sem("s_pe"), is_pe=True)
        AC = Eng("act", sem("s_act"))
        DV = Eng("dve", sem("s_dve"))
        PO = Eng("pool", sem("s_pool"))
        SP = Eng("sp", sem("s_sp"))
        wslots = [DmaSlot(sem("s_w%d" % i), "w%d" % i) for i in range(4)]
        xslots = [DmaSlot(sem("s_x%d" % i), "x%d" % i) for i in range(3)]
        cslot = DmaSlot(sem("s_c"), "c")
        tslot = DmaSlot(sem("s_t"), "t")
        mslot = DmaSlot(sem("s_m"), "m")
        oslots = [DmaSlot(sem("s_o%d" % i), "o%d" % i) for i in range(4)]
        dslots = [DmaSlot(sem("s_d%d" % i), "d%d" % i) for i in range(8)] if debug else []
        dstate = {"i": 0}

        def next_dslot():
            d = dslots[dstate["i"]]
            dstate["i"] += 1
            return d

        r_u = Res(); r_qcs = Res(); r_ot = Res(); r_kt = Res(); r_va = Res()
        r_dg = [Res() for _ in range(6)]
        r_cb = [Res(), Res()]; r_sg = [Res(), Res()]; r_qf = [Res(), Res()]
        r_rt = [Res(), Res()]; r_et = [Res() for _ in range(4)]; r_pt = [Res() for _ in range(8)]
        r_pm = [Res() for _ in range(4)]
        r_y = [Res() for _ in range(NCC)]
        r_mg = Res()
        r_act = [Res(), Res()]
        r_h1 = [Res() for _ in range(KC)]
        r_sq = [Res(), Res()]
        r_rstd = Res()
        r_st = [Res() for _ in range(3)]
        r_ft = [Res() for _ in range(4)]
        r_bank = [Res() for _ in range(8)]
        r_cst = Res(); r_const = Res()
        r_tab = Res(); r_msk = Res()
        state = {"bank": 0, "ft": 0, "x": 0, "sq": 0, "w": 0, "o": 0}

        pinned = set()

        def next_bank():
            b = state["bank"]
            while b in pinned:
                b = (b + 1) % 6
            state["bank"] = (b + 1) % 6
            return banks[b], r_bank[b]

        def bank_index(bk_res):
            return r_bank.index(bk_res)

        def next_ft():
            i = state["ft"]
            state["ft"] = (i + 1) % 4
            return FT[i], r_ft[i]

        woffs = np.cumsum([0] + [n for _, n in sched])
        wstate = {"issued": 0, "used": 0, "goff": 0}
        nunits = len(sched)
        total_units = nunits * ng

        def w_issue_upto(k):
            while wstate["issued"] < min(k, total_units):
                i = wstate["issued"]
                u = i % nunits
                slot = wslots[i % 4]
                n = sched[u][1]
                dma(PO, slot, WS[i % 4][:, 0:n], wts_d[:, int(woffs[u]):int(woffs[u]) + n],
                    writes=[slot.res])
                wstate["issued"] += 1

        def w_next(name):
            i = wstate["used"]
            u = i % nunits
            assert sched[u][0] == name, (sched[u][0], name)
            w_issue_upto(i + 3)
            wstate["used"] += 1
            return WS[i % 4], wslots[i % 4].res, i

        def w_done():
            w_issue_upto(wstate["used"] + 3)

        def mm_chain(out_ap, pairs, reads, bres, extra=()):
            n = len(pairs)

            def fn(h):
                ins = None
                for i, (l, r) in enumerate(pairs):
                    ins = h.matmul(out_ap, l, r, start=(i == 0), stop=(i == n - 1))
                return ins
            return op(PE, fn, reads=reads, writes=[bres], extra=extra)

        def act(out_ap, in_ap, func, reads, writes, bias=None, scale=None, extra=()):
            kw = {}
            if bias is not None:
                kw["bias"] = bias
            if scale is not None:
                kw["scale"] = scale
            return op(AC, lambda h: h.activation(out_ap, in_ap, func, **kw), reads=reads, writes=writes, extra=extra)

        def tt(out_ap, a, b, alu, reads, writes, extra=()):
            return op(DV, lambda h: h.tensor_tensor(out_ap, a, b, alu), reads=reads, writes=writes, extra=extra)

        def stt(out_ap, in0, scalar, in1, op0, op1, reads, writes, extra=()):
            return op(DV, lambda h: h.scalar_tensor_tensor(out_ap, in0, scalar, in1, op0, op1),
                      reads=reads, writes=writes, extra=extra)

        def ts(out_ap, in0, s1, s2, op0, op1, reads, writes, extra=()):
            if s2 is None:
                return op(DV, lambda h: h.tensor_scalar(out_ap, in0, s1, None, op0), reads=reads, writes=writes, extra=extra)
            return op(DV, lambda h: h.tensor_scalar(out_ap, in0, s1, s2, op0, op1), reads=reads, writes=writes, extra=extra)

        def cc_(c0, n=1):
            return CST[:, c0:c0 + n]

        dma(SP, cslot, CST[:, :], cst_d[:, :], writes=[r_cst])
        dma(SP, cslot, BV[:, :], bv_d[:, :], writes=[r_cst])
        op(DV, lambda h: h.memset(ONEB[:, :], 1.0), writes=[r_const])
        op(DV, lambda h: h.memset(ONEF[:, :], 1.0), writes=[r_const])
        op(DV, lambda h: h.memset(ONELH[:, :, :], 0.0), writes=[r_const])
        op(DV, lambda h: h.memset(ONELH[:, 0, 0:64], 1.0), writes=[r_const])
        op(DV, lambda h: h.memset(ONELH[:, 1, 64:128], 1.0), writes=[r_const])
        act(ESK[:, :], cc_(C_SK, 16), AF.Exp, reads=[r_cst], writes=[r_const])
        op(DV, lambda h: h.tensor_copy(IDB[:, :], CST[:, C_ID:C_ID + 128]), reads=[r_cst], writes=[r_const])


        EPSB = sb("EPSB", [128, 1], F32)
        CARRY = sb("CARRY", [128, NCC, 32], BF16)
        r_carry = [Res() for _ in range(NCC)]
        if debug:
            print("sbuf bytes remaining", nc.sbuf_bytes_remaining)
        op(DV, lambda h: h.memset(EPSB[:, :], EPS), writes=[r_const])

        def mm1(out_ap, l, r, st, sp_):
            return lambda h: h.matmul(out_ap, l, r, start=st, stop=sp_)

        def recip(ap, reads, writes):
            return op(DV, lambda h: h.reciprocal(ap, ap), reads=reads, writes=writes)

        def xload(src_ap, n):
            xi = state["x"]
            state["x"] = (xi + 1) % 3
            dma(SP, xslots[xi], XR[xi][:, 0:n], src_ap, writes=[xslots[xi].res])
            return XR[xi], xslots[xi].res

        def next_sq():
            si = state["sq"]
            state["sq"] ^= 1
            return SQ[si], r_sq[si]

        def wtile(W, i):
            return W[:, i * 128:(i + 1) * 128]

        fn_q = []
        fn_done = set()

        def emit_fn(n):
            for _ in range(min(n, len(fn_q))):
                fn_q.pop(0)()

        def stats_chunk(gg, kc):
            xr, xres = xload(xT_d[gg, :, kc, :], TS)
            sq, sqr = next_sq()
            act(sq[:, :], xr[:, :], AF.Square, reads=[xres], writes=[sqr])
            op(PE, mm1(banks[6][:, :], ONEB[:, :], sq[:, 0:512], kc == 0, kc == KC - 1),
               reads=[sqr, r_const], writes=[r_bank[6]])
            op(PE, mm1(banks[7][:, 0:144], ONEB[:, :], sq[:, 512:656], kc == 0, kc == KC - 1),
               reads=[sqr, r_const], writes=[r_bank[7]])

        def stats_finish():
            act(RSTD[:, 0:512], banks[6][:, :], AF.Ln, reads=[r_bank[6], r_const], writes=[r_rstd],
                bias=EPSB[:, 0:1], scale=1.0 / D)
            act(RSTD[:, 512:656], banks[7][:, 0:144], AF.Ln, reads=[r_bank[7], r_const], writes=[r_rstd],
                bias=EPSB[:, 0:1], scale=1.0 / D)
            act(RSTD[:, :], RSTD[:, :], AF.Exp, reads=[r_rstd], writes=[r_rstd], scale=-0.5)

        def u_chunk(gg, kc):
            xr, xres = xload(xT_d[gg, :, kc, :], TS)
            stt(U[:, kc, :], xr[:, :], cc_(C_G1 + kc), RSTD[:, :], ALU.mult, ALU.mult,
                reads=[xres, r_rstd, r_cst], writes=[r_u])

        for g in range(ng):
            emit_fn(6)
            pro_tokens = []
            if g == 0:
                pslots = [DmaSlot(sem("s_p%d" % q), "p%d" % q) for q in range(4)]
                xviews = []
                for q in range(4):
                    base = R1[:, q * 8 * TS:(q + 1) * 8 * TS] if q < 3 else R2[:, 0:8 * TS]
                    v = base.rearrange("p (a b) -> p a b", b=TS)
                    dma(SP, pslots[q], v, xT_d[0, :, 8 * q:8 * q + 8, :], writes=[pslots[q].res])
                    xviews.extend((v[:, i, :], pslots[q].res) for i in range(8))
                for kc in range(KC):
                    xv, xvr = xviews[kc]
                    sq, sqr = next_sq()
                    act(sq[:, :], xv, AF.Square, reads=[xvr], writes=[sqr])
                    op(PE, mm1(banks[6][:, :], ONEB[:, :], sq[:, 0:512], kc == 0, kc == KC - 1),
                       reads=[sqr, r_const], writes=[r_bank[6]])
                    op(PE, mm1(banks[7][:, 0:144], ONEB[:, :], sq[:, 512:656], kc == 0, kc == KC - 1),
                       reads=[sqr, r_const], writes=[r_bank[7]])
                stats_finish()
                for kc in range(KC):
                    xv, xvr = xviews[kc]
                    stt(U[:, kc, :], xv, cc_(C_G1 + kc), RSTD[:, :], ALU.mult, ALU.mult,
                        reads=[xvr, r_rstd, r_cst], writes=[r_u])
                pro_tokens = res_tokens([p_.res for p_ in pslots])
            fenceR1 = res_tokens(r_h1) + pro_tokens
            fenceQF = res_tokens(r_h1[28:31]) + pro_tokens
            fenceR2 = res_tokens(r_act) + res_tokens([r_mg]) + pro_tokens
            dma(SP, tslot, TAB[:, :, :], tabs_d[g], writes=[r_tab], extra=fenceR2)
            dma(SP, tslot, CMK[:, :], cmask_d[g], writes=[r_tab])
            dma(PO, mslot, MSK[:, :], masks_d[g], writes=[r_msk])
            if debug and g == 0:
                dma(SP, next_dslot(), dbg["d_u"], U[:, :, :].rearrange("p a b -> p (a b)"), reads=[r_u])

            def rope_finish(dst_ap, dres, qf, qfr, segs, tok_lo, tok_hi, dfence=None):
                outs = []
                for (lo, hi) in segs:
                    b2, b2r = next_bank()
                    op(PE, mm1(b2[:, 0:hi - lo], CST[:, C_PERM:C_PERM + 128], qf[:, lo:hi], True, True),
                       reads=[qfr, r_cst], writes=[b2r])
                    outs.append((b2, b2r, lo, hi))
                n = tok_hi - tok_lo
                tt(RT[0][:, 0:n], qf[:, 0:n], TAB[:, 0, tok_lo:tok_hi], ALU.mult,
                   reads=[qfr, r_tab], writes=[r_rt[0]], extra=fenceR2)
                for (b2, b2r, lo, hi) in outs:
                    tt(RT[1][:, lo:hi], b2[:, 0:hi - lo], TAB[:, 1, tok_lo + lo:tok_lo + hi], ALU.mult,
                       reads=[b2r, r_tab], writes=[r_rt[1]], extra=fenceR2)
                tt(dst_ap, RT[0][:, 0:n], RT[1][:, 0:n], ALU.add, reads=[r_rt[0], r_rt[1]], writes=[dres],
                   extra=fenceR1 if dfence is None else dfence)

            def rope_q(jt, qi):
                assert g == 0 or not fn_q or (jt // 2) in fn_done, (jt, sorted(fn_done))
                rope_finish(QT[:, jt, :], r_qcs, QF[qi], r_qf[qi], [(0, 512)], 128, 640, res_tokens([r_h1[jt // 2]]) + pro_tokens)

            pend = None
            for jt in range(16):
                W, wres, _ = w_next("q%d" % jt)
                bk, br = next_bank()
                mm_chain(bk[:, :], [(wtile(W, kc), U[:, kc, 128:640]) for kc in range(KC)],
                         reads=[wres, r_u], bres=br)
                w_done()
                qi = jt % 2
                act(QF[qi][:, 0:512], bk[:, :], AF.Identity, reads=[br, r_cst], writes=[r_qf[qi]],
                    bias=cc_(C_BIN + jt), extra=fenceQF)
                if pend is not None:
                    rope_q(*pend)
                pend = (jt, qi)
                emit_fn(2)
            rope_q(*pend)
            emit_fn(len(fn_q))
            fenceR1 = res_tokens(r_h1) + pro_tokens
            for kp in range(2):
                W, wres, _ = w_next("k%d" % kp)
                bA, brA = next_bank()
                bB, brB = next_bank()
                mm_chain(bA[:, :], [(wtile(W, kc), U[:, kc, 0:512]) for kc in range(KC)], reads=[wres, r_u], bres=brA)
                mm_chain(bB[:, 0:144], [(wtile(W, kc), U[:, kc, 512:656]) for kc in range(KC)], reads=[wres, r_u], bres=brB)
                w_done()
                qi = kp % 2
                act(QF[qi][:, 0:512], bA[:, :], AF.Identity, reads=[brA, r_cst], writes=[r_qf[qi]],
                    bias=cc_(C_BIN + 16 + kp), extra=fenceR1)
                act(QF[qi][:, 512:656], bB[:, 0:144], AF.Identity, reads=[brB, r_cst], writes=[r_qf[qi]],
                    bias=cc_(C_BIN + 16 + kp), extra=fenceR1)
                rope_finish(KT[:, kp, :], r_kt, QF[qi], r_qf[qi], [(0, 512), (512, 656)], 0, 656)

            op(DV, lambda h: h.memset(VA_flat[:, :], 0.0), writes=[r_va], extra=fenceR1)
            for u in range(2):
                W, wres, _ = w_next("v%d" % u)
                bA, brA = next_bank()
                bB, brB = next_bank()
                outs = []
                for tb in range(6):
                    M = 128 if tb < 5 else 16
                    t0 = tb * 128
                    if tb < 4:
                        o_ap, o_r = bA[0:M, tb * 128:(tb + 1) * 128], brA
                    else:
                        o_ap, o_r = bB[0:M, (tb - 4) * 128:(tb - 3) * 128], brB
                    mm_chain(o_ap, [(U[:, kc, t0:t0 + M], wtile(W, kc)) for kc in range(KC)],
                             reads=[wres, r_u], bres=o_r)
                    outs.append((o_ap, o_r, M))
                w_done()
                for tb, (o_ap, o_r, M) in enumerate(outs):
                    for hh in range(2):
                        hk = 2 * u + hh
                        tt(VA[0:M, tb, hk, 0, 0:64], o_ap[:, hh * 64:(hh + 1) * 64], BV[0:M, hk * 64:(hk + 1) * 64],
                           ALU.add, reads=[o_r, r_cst], writes=[r_va])
                        tt(VA[0:M, tb, hk, 1, 64:128], o_ap[:, hh * 64:(hh + 1) * 64], BV[0:M, hk * 64:(hk + 1) * 64],
                           ALU.add, reads=[o_r, r_cst], writes=[r_va])

            def att_scores(blk, hk):
                P0 = (hk % 2) * 64
                kp = hk // 2
                tbase = kp * 8
                keyspec = [(KT[P0:P0 + 64, kp, blk * 128:(blk + 1) * 128], 128, blk, 0 if blk == 0 else 1),
                           (KT[P0:P0 + 64, kp, (blk + 1) * 128:(blk + 2) * 128], 128, blk + 1, 2),
                           (KT[P0:P0 + 64, kp, 640:656], 16, 5, None)]
                ptiles = []
                par = (blk * 4 + hk) % 2
                for ki, (kap, M, tbi, mi) in enumerate(keyspec):
                    for half in range(2):
                        bk, br = next_bank()
                        rhs = QT[P0:P0 + 64, tbase + half * 4:tbase + half * 4 + 4, blk * 128:(blk + 1) * 128]
                        o3 = bk[0:M, :].rearrange("p (a b) -> p a b", a=4)
                        mm_chain(o3, [(kap, rhs)], reads=[r_kt, r_qcs], bres=br)
                        if mi is not None:
                            ei = ki * 2 + half
                            act(ET[ei][:, :], bk[:, :], AF.Exp, reads=[br], writes=[r_et[ei]], scale=0.125,
                                extra=fenceR2)
                            pi = ei + 4 * par
                            tt(PT[pi][:, :], ET[ei][:, :], MSK[:, mi * 512:(mi + 1) * 512], ALU.mult,
                               reads=[r_et[ei], r_msk], writes=[r_pt[pi]], extra=fenceR2)
                            ptiles.append((PT[pi][:, :], r_pt[pi], 128, tbi, half))
                        else:
                            mi_ = half + 2 * par
                            act(PM[mi_][0:16, :], bk[0:16, :], AF.Exp, reads=[br], writes=[r_pm[mi_]], scale=0.125,
                                extra=fenceR2)
                            ptiles.append((PM[mi_][0:16, :], r_pm[mi_], 16, 5, half))
                return ptiles

            def att_pv(blk, hk, ptiles):
                bO, brO = next_bank()
                bD, brD = next_bank()
                for i, (pap, pres, M, tbi, half) in enumerate(ptiles):
                    op(PE, mm1(bO[:, :], VA[0:M, tbi, hk, half, :], pap, i == 0, i == 5),
                       reads=[pres, r_va], writes=[brO])
                for i, (pap, pres, M, tbi, half) in enumerate(ptiles):
                    op(PE, mm1(bD[:, :], ONELH[0:M, half, :], pap, i == 0, i == 5),
                       reads=[pres, r_const], writes=[brD])
                f, fr = next_ft()
                for c in range(4):
                    act(f[:, c * 128:(c + 1) * 128], bD[:, c * 128:(c + 1) * 128], AF.Ln, reads=[brD, r_const], writes=[fr],
                        bias=ESK[:, hk * 4 + c:hk * 4 + c + 1])
                act(f[:, :], f[:, :], AF.Exp, reads=[fr], writes=[fr], scale=-1.0)
                tt(OT[:, hk * 4:hk * 4 + 4, blk * 128:(blk + 1) * 128], bO[:, :].rearrange("p (c q) -> p c q", c=4),
                   f[:, :].rearrange("p (c q) -> p c q", c=4), ALU.mult, reads=[brO, fr], writes=[r_ot], extra=fenceR1)

            its = [(blk, hk) for blk in range(4) for hk in range(4)]
            att_state = {"i": 0, "prev": None}

            def att_step():
                i = att_state["i"]
                if i < len(its):
                    blk, hk = its[i]
                    pt_ = att_scores(blk, hk)
                    if att_state["prev"] is not None:
                        att_pv(*att_state["prev"])
                    att_state["prev"] = (blk, hk, pt_)
                    att_state["i"] = i + 1
                elif att_state["prev"] is not None:
                    att_pv(*att_state["prev"])
                    att_state["prev"] = None

            fence_y_lo = res_tokens(r_rt + [r_tab])

            def fence_y_for(cc):
                if cc < 8:
                    return fence_y_lo
                assert att_state["i"] == len(its) and att_state["prev"] is None
                return fence_y_lo + res_tokens(r_et + r_pt + r_pm)

            def conv_pe(cc, ci):
                bk, br = next_bank()
                for k in range(CW):
                    di = dgs["i"]
                    dgs["i"] = (di + 1) % NDG
                    ts(DG[di][:, :], IDB[:, :], cc_(C_CW + cc * CW + k), None, ALU.mult, None,
                       reads=[r_const, r_cst], writes=[r_dg[di]], extra=fenceR1)
                    op(PE, mm1(bk[:, :], DG[di][:, :], CB[ci][:, 98 + k:610 + k], k == 0, k == CW - 1),
                       reads=[r_dg[di], r_cb[ci]], writes=[br])
                act(Y[:, cc, :], bk[:, :], AF.Identity, reads=[br, r_cst], writes=[r_y[cc]], bias=cc_(C_CB + cc),
                    extra=fence_y_for(cc))
                f, fr = next_ft()
                fsq = f[:, 0:256].bitcast(BF16)
                fy = f[:, 256:512].bitcast(BF16)
                act(fsq, Y[:, cc, :], AF.Square, reads=[r_y[cc]], writes=[fr])
                act(fy, Y[:, cc, :], AF.Copy, reads=[r_y[cc]], writes=[fr])
                if stq:
                    ln_stat_mm(*stq.pop(0))
                stq.append((cc, (fy, fsq), fr))

            def ln_stat_mm(cc, f2, fr):
                op(PE, mm1(banks[6][:, :], ONEB[:, :], f2[0], cc == 0, cc == NCC - 1),
                   reads=[fr, r_const], writes=[r_bank[6]])
                op(PE, mm1(banks[7][:, :], ONEB[:, :], f2[1], cc == 0, cc == NCC - 1),
                   reads=[fr, r_const], writes=[r_bank[7]])

            stq = []
            dgs = {"i": 0}
            pendc = None
            for cc in range(NCC):
                ci = cc % 2
                ua = C_BIN + 20 + 2 * cc
                ug = ua + 1
                if g == 0:
                    W, wres, _ = w_next("ca%d" % cc)
                    bA0, rA0 = next_bank()
                    bA1, rA1 = next_bank()
                    mm_chain(bA0[:, :], [(wtile(W, kc), U[:, kc, 0:512]) for kc in range(KC)], reads=[wres, r_u], bres=rA0)
                    mm_chain(bA1[:, 0:128], [(wtile(W, kc), U[:, kc, 512:640]) for kc in range(KC)], reads=[wres, r_u], bres=rA1)
                    w_done()
                    pinned.update([bank_index(rA0), bank_index(rA1)])
                    att_step()
                    W, wres, _ = w_next("cg%d" % cc)
                    bG0, rG0 = next_bank()
                    bG1, rG1 = next_bank()
                    mm_chain(bG0[:, :], [(wtile(W, kc), U[:, kc, 0:512]) for kc in range(KC)], reads=[wres, r_u], bres=rG0)
                    mm_chain(bG1[:, 0:128], [(wtile(W, kc), U[:, kc, 512:640]) for kc in range(KC)], reads=[wres, r_u], bres=rG1)
                    w_done()
                    act(SG[ci][:, 0:512], bG0[:, :], AF.Sigmoid, reads=[rG0, r_cst], writes=[r_sg[ci]], bias=cc_(ug), extra=fenceR1)
                    act(SG[ci][:, 512:640], bG1[:, 0:128], AF.Sigmoid, reads=[rG1, r_cst], writes=[r_sg[ci]], bias=cc_(ug), extra=fenceR1)
                    if pendc is not None:
                        conv_pe(*pendc)
                    stt(CB[ci][:, 0:512], bA0[:, :], cc_(ua), SG[ci][:, 0:512], ALU.add, ALU.mult,
                        reads=[rA0, r_sg[ci], r_cst], writes=[r_cb[ci]], extra=fenceR1)
                    stt(CB[ci][:, 512:640], bA1[:, 0:128], cc_(ua), SG[ci][:, 512:640], ALU.add, ALU.mult,
                        reads=[rA1, r_sg[ci], r_cst], writes=[r_cb[ci]], extra=fenceR1)
                    tt(CB[ci][:, 0:128], CB[ci][:, 0:128], CMK[:, :], ALU.mult, reads=[r_cb[ci], r_tab], writes=[r_cb[ci]])
                    pinned.clear()
                else:
                    W, wres, _ = w_next("ca%d" % cc)
                    bA0, rA0 = next_bank()
                    mm_chain(bA0[:, :], [(wtile(W, kc), U[:, kc, 128:640]) for kc in range(KC)], reads=[wres, r_u], bres=rA0)
                    w_done()
                    pinned.add(bank_index(rA0))
                    att_step()
                    W, wres, _ = w_next("cg%d" % cc)
                    bG0, rG0 = next_bank()
                    mm_chain(bG0[:, :], [(wtile(W, kc), U[:, kc, 128:640]) for kc in range(KC)], reads=[wres, r_u], bres=rG0)
                    w_done()
                    act(SG[ci][:, 128:640], bG0[:, :], AF.Sigmoid, reads=[rG0, r_cst], writes=[r_sg[ci]], bias=cc_(ug), extra=fenceR1)
                    if pendc is not None:
                        conv_pe(*pendc)
                    op(DV, lambda h, cc=cc, ci=ci: h.tensor_copy(CB[ci][:, 98:128], CARRY[:, cc, 0:30]),
                       reads=[r_carry[cc]], writes=[r_cb[ci]], extra=fenceR1)
                    stt(CB[ci][:, 128:640], bA0[:, :], cc_(ua), SG[ci][:, 128:640], ALU.add, ALU.mult,
                        reads=[rA0, r_sg[ci], r_cst], writes=[r_cb[ci]], extra=fenceR1)
                    pinned.clear()
                if g + 1 < ng:
                    op(DV, lambda h, cc=cc, ci=ci: h.tensor_copy(CARRY[:, cc, 0:30], CB[ci][:, 610:640]),
                       reads=[r_cb[ci]], writes=[r_carry[cc]])
                pendc = (cc, ci)
                att_step()
            conv_pe(*pendc)
            assert att_state["i"] == len(its) and att_state["prev"] is None
            if debug and g == 0:
                dma(SP, next_dslot(), dbg["d_q"], QT[:, :, :].rearrange("p a b -> p (a b)"), reads=[r_qcs])
                dma(SP, next_dslot(), dbg["d_k"], KT[:, :, :].rearrange("p a b -> p (a b)"), reads=[r_kt])
                dma(SP, next_dslot(), dbg["d_ot"], OT[:, :, :].rearrange("p a b -> p (a b)"), reads=[r_ot])
            while stq:
                ln_stat_mm(*stq.pop(0))
            ts(ST[0][:, :], banks[6][:, :], 1.0 / CD, None, ALU.mult, None, reads=[r_bank[6]], writes=[r_st[0]])
            tt(ST[2][:, :], ST[0][:, :], ST[0][:, :], ALU.mult, reads=[r_st[0]], writes=[r_st[2]])
            stt(ST[2][:, :], banks[7][:, :], 1.0 / CD, ST[2][:, :], ALU.mult, ALU.subtract,
                reads=[r_bank[7], r_st[2]], writes=[r_st[2]])
            act(ST[1][:, :], ST[2][:, :], AF.Ln, reads=[r_st[2], r_const], writes=[r_st[1]], bias=EPSB[:, 0:1], scale=1.0)
            act(ST[1][:, :], ST[1][:, :], AF.Exp, reads=[r_st[1]], writes=[r_st[1]], scale=-0.5)
            if debug and g == 0:
                dma(SP, next_dslot(), dbg["d_y"], Y[:, :, :].rearrange("p a b -> p (a b)"), reads=r_y)
            for cc in range(NCC):
                f, fr = next_ft()
                tt(f[:, :], Y[:, cc, :], ST[0][:, :], ALU.subtract, reads=[r_y[cc], r_st[0]], writes=[fr])
                tt(f[:, :], f[:, :], ST[1][:, :], ALU.mult, reads=[fr, r_st[1]], writes=[fr])
                act(CSB[:, cc, :], f[:, :], AF.Silu, reads=[fr, r_cst], writes=[r_qcs],
                    bias=cc_(C_LB + cc), scale=cc_(C_LG + cc))
            if debug and g == 0:
                dma(SP, next_dslot(), dbg["d_cs"], CSB[:, :, :].rearrange("p a b -> p (a b)"), reads=[r_qcs])

            fence_mg = res_tokens(r_y)
            for j in range(KC):
                W, wres, _ = w_next("ga%d" % j)
                bk, br = next_bank()
                mm_chain(bk[:, :], [(wtile(W, kc), U[:, kc, 128:640]) for kc in range(KC)], reads=[wres, r_u], bres=br)
                w_done()
                fa, fra = next_ft()
                act(fa[:, :], bk[:, :], AF.Sigmoid, reads=[br, r_cst], writes=[fra], bias=cc_(C_BIN + 52 + 2 * j))
                W, wres, _ = w_next("gb%d" % j)
                bk, br = next_bank()
                mm_chain(bk[:, :], [(wtile(W, kc), U[:, kc, 128:640]) for kc in range(KC)], reads=[wres, r_u], bres=br)
                w_done()
                fb, frb = next_ft()
                act(fb[:, :], bk[:, :], AF.Sigmoid, reads=[br, r_cst], writes=[frb], bias=cc_(C_BIN + 53 + 2 * j))
                W, wres, _ = w_next("ao%d" % j)
                bk, br = next_bank()
                mm_chain(bk[:, :], [(wtile(W, oc), OT[:, oc, :]) for oc in range(16)], reads=[wres, r_ot], bres=br)
                w_done()
                tt(fa[:, :], bk[:, :], fa[:, :], ALU.mult, reads=[br, fra], writes=[fra])
                W, wres, _ = w_next("co%d" % j)
                bk, br = next_bank()
                mm_chain(bk[:, :], [(wtile(W, cc), CSB[:, cc, :]) for cc in range(NCC)], reads=[wres, r_qcs], bres=br)
                w_done()
                stt(fb[:, :], bk[:, :], cc_(C_BCO + j), fb[:, :], ALU.add, ALU.mult, reads=[br, frb, r_cst], writes=[frb])
                tt(MG[:, j, :], fa[:, :], fb[:, :], ALU.add, reads=[fra, frb], writes=[r_mg], extra=fence_mg)
            if debug and g == 0:
                dma(SP, next_dslot(), dbg["d_mg"], MG[:, :, :].rearrange("p a b -> p (a b)"), reads=[r_mg])

            fence_h1 = res_tokens([r_qcs, r_ot, r_kt, r_va] + r_cb + r_sg + r_qf + r_dg)
            pend = None
            for j in range(KC):
                W, wres, _ = w_next("wo%d" % j)
                bk, br = next_bank()
                mm_chain(bk[:, :], [(wtile(W, kc), MG[:, kc, :]) for kc in range(KC)], reads=[wres, r_mg], bres=br)
                w_done()
                xr, xres = xload(xT_d[g, :, j, 128:640], 512)
                tt(H1[:, j, :], bk[:, :], xr[:, 0:512], ALU.add, reads=[br, xres], writes=[r_h1[j]], extra=fence_h1)
                sq, sqr = next_sq()
                act(sq[:, 0:512], H1[:, j, :], AF.Square, reads=[r_h1[j]], writes=[sqr])
                if pend is not None:
                    op(PE, mm1(banks[6][:, :], ONEB[:, :], pend[0][:, 0:512], pend[2] == 0, False),
                       reads=[pend[1], r_const], writes=[r_bank[6]])
                pend = (sq, sqr, j)
            op(PE, mm1(banks[6][:, :], ONEB[:, :], pend[0][:, 0:512], False, True), reads=[pend[1], r_const], writes=[r_bank[6]])
            act(ST[0][:, :], banks[6][:, :], AF.Ln, reads=[r_bank[6], r_const], writes=[r_st[0]], bias=EPSB[:, 0:1], scale=1.0 / D)
            act(ST[0][:, :], ST[0][:, :], AF.Exp, reads=[r_st[0]], writes=[r_st[0]], scale=-0.5)
            if debug and g == 0:
                dma(SP, next_dslot(), dbg["d_h1"], H1[:, :, :].rearrange("p a b -> p (a b)"), reads=r_h1)
            for kc in range(KC):
                stt(U[:, kc, 0:512], H1[:, kc, :], cc_(C_G2 + kc), ST[0][:, :], ALU.mult, ALU.mult,
                    reads=[r_h1[kc], r_st[0], r_cst], writes=[r_u])

            fence_act = res_tokens([r_mg])
            f0 = 0
            nxt = g + 1 if g + 1 < ng else None
            sidx = 0
            for fg, nf in enumerate(FG_SIZES):
                ai = fg % 2
                last = (fg == len(FG_SIZES) - 1)
                for t in range(nf):
                    W, wres, _ = w_next("fgate%d" % (f0 + t))
                    bg, brg = next_bank()
                    mm_chain(bg[:, :], [(wtile(W, kc), U[:, kc, 0:512]) for kc in range(KC)], reads=[wres, r_u], bres=brg)
                    w_done()
                    W, wres, _ = w_next("fup%d" % (f0 + t))
                    bu, bru = next_bank()
                    mm_chain(bu[:, :], [(wtile(W, kc), U[:, kc, 0:512]) for kc in range(KC)], reads=[wres, r_u], bres=bru)
                    w_done()
                    f, fr = next_ft()
                    act(f[:, :], bg[:, :], AF.Silu, reads=[brg], writes=[fr])
                    tt(ACT[ai][:, t, :], f[:, :], bu[:, :], ALU.mult, reads=[fr, bru], writes=[r_act[ai]], extra=fence_act)
                    if nxt is not None and sidx < KC:
                        stats_chunk(nxt, sidx)
                        sidx += 1
                        if sidx == KC:
                            stats_finish()
                pendq = []
                for j in range(KC):
                    W, wres, _ = w_next("wd%d_%d" % (fg, j))
                    bk, br = next_bank()
                    mm_chain(bk[:, :], [(wtile(W, t), ACT[ai][:, t, :]) for t in range(nf)], reads=[wres, r_act[ai]], bres=br)
                    w_done()
                    tt(H1[:, j, :], bk[:, :], H1[:, j, :], ALU.add, reads=[br, r_h1[j]], writes=[r_h1[j]])
                    if last:
                        sq, sqr = next_sq()
                        act(sq[:, 0:512], H1[:, j, :], AF.Square, reads=[r_h1[j]], writes=[sqr])
                        pendq.append((sq, sqr, j))
                        if len(pendq) > 1:
                            p_ = pendq.pop(0)
                            op(PE, mm1(banks[6][:, :], ONEB[:, :], p_[0][:, 0:512], p_[2] == 0, False),
                               reads=[p_[1], r_const], writes=[r_bank[6]])
                        if nxt is not None:
                            u_chunk(nxt, j)
                if last:
                    while pendq:
                        p_ = pendq.pop(0)
                        op(PE, mm1(banks[6][:, :], ONEB[:, :], p_[0][:, 0:512], p_[2] == 0, len(pendq) == 0),
                           reads=[p_[1], r_const], writes=[r_bank[6]])
                f0 += nf

            act(ST[1][:, :], banks[6][:, :], AF.Ln, reads=[r_bank[6], r_const], writes=[r_st[1]], bias=EPSB[:, 0:1], scale=1.0 / D)
            act(ST[1][:, :], ST[1][:, :], AF.Exp, reads=[r_st[1]], writes=[r_st[1]], scale=-0.5)
            def fn_op(g_, j):
                def run():
                    i = state["ft"]
                    f, fr = next_ft()
                    stt(f[:, :], H1[:, j, :], cc_(C_GF + j), ST[1][:, :], ALU.mult, ALU.mult,
                        reads=[r_h1[j], r_st[1], r_cst], writes=[fr])
                    dma(SP, oslots[i], out_d[g_, j * 128:(j + 1) * 128, :], f[:, :], reads=[fr])
                    fn_done.add(j)
                return run
            fn_q.extend(fn_op(g, j) for j in list(range(28, KC)) + list(range(28)))
            fn_done.clear()
            if g == ng - 1:
                emit_fn(len(fn_q))

        final = [Tok(s.sem, 16 * s.cnt, None, "D" + s.name) for s in oslots + dslots if s.cnt > 0]
        SP.record(None, final, inc=False)

        with nc.Block() as block:
            @block.tensor
            def _(h):
                PE.replay(h)

            @block.scalar
            def _(h):
                AC.replay(h)

            @block.vector
            def _(h):
                DV.replay(h)

            @block.gpsimd
            def _(h):
                PO.replay(h)

            @block.sync
            def _(h):
                SP.replay(h)
    return nc


_NC_CACHE = {}


def kernel(x, meta_tokens, mix_norm_g, w_in, b_in, attn_sinks, conv_w, conv_b, conv_ln_g, conv_ln_b,
           w_attn_o, w_conv_o, b_conv_o, w_out, ffn_norm_g, w_gate_up, w_down, final_norm_g):
    f = lambda a: np.asarray(a, dtype=np.float32)
    x = f(x); meta = f(meta_tokens)
    wts = pack_weights(f(w_in)[0], f(w_attn_o)[0], f(w_conv_o)[0], f(w_out)[0], f(w_gate_up)[0], f(w_down)[0])
    cst = pack_consts(f(b_in)[0], f(b_conv_o)[0], f(mix_norm_g)[0], f(ffn_norm_g)[0], f(final_norm_g),
                      f(conv_w)[0], f(conv_b)[0], f(conv_ln_g)[0], f(conv_ln_b)[0], f(attn_sinks)[0])
    bv = np.ascontiguousarray(np.broadcast_to(f(b_in)[0][QD + KVD:QD + 2 * KVD][None, :], (128, 256)))
    if "nc" not in _NC_CACHE:
        _NC_CACHE["nc"] = build_nc()
    nc = _NC_CACHE["nc"]
    in_maps = []
    for core in range(NCORES):
        m = core_inputs(core, x, meta)
        m.update({"cst": cst, "bv": bv, "wts": wts})
        in_maps.append(m)
    res = run_bass_kernel_spmd(nc, in_maps, core_ids=list(range(NCORES)))
    out = np.empty((BATCH, SEQ, D), np.float32)
    for core in range(NCORES):
        b, half = core // 2, core % 2
        o = np.asarray(res.results[core]["outT"])
        for g in range(NG):
            s0 = half * 2048 + g * T
            out[b, s0:s0 + T, :] = o[g].T
    return out
```

```python
import numpy as np
import concourse.bass as bass
import concourse.mybir as mybir
from concourse.bass_utils import run_bass_kernel_spmd

F32 = mybir.dt.float32
BF16 = mybir.dt.bfloat16
AF = mybir.ActivationFunctionType
ALU = mybir.AluOpType

D = 4096
SEQ = 4096
BATCH = 4
N_META = 16
HD = 64
NQH = 32
NKV = 4
QD = 2048
KVD = 256
CD = 2048
CW = 31
FFN = 11008
IN_DIM = QD + 2 * KVD + 2 * CD + 2 * D
EPS = 1e-6
T = 512
NG = 4
HALO = 128
TS = HALO + T + N_META
KC = D // 128
NCC = CD // 128
NFT = FFN // 128
FG_SIZES = [11, 11, 11, 11, 11, 11, 10, 10]
NCORES = 8

SAME_ENGINE_SYNC = True

C_BIN = 0
C_BCO = C_BIN + 116
C_G1 = C_BCO + 32
C_G2 = C_G1 + 32
C_GF = C_G2 + 32
C_CW = C_GF + 32
C_CB = C_CW + NCC * CW
C_LG = C_CB + NCC
C_LB = C_LG + NCC
C_SK = C_LB + NCC
C_PERM = C_SK + 16
C_ID = C_PERM + 128
NCST = C_ID + 128


def in_units():
    units = []
    for jt in range(16):
        pair, j = jt // 8, jt % 8
        hlo = (2 * pair) * 8 + j
        hhi = (2 * pair + 1) * 8 + j
        cols = np.concatenate([hlo * 64 + np.arange(64), hhi * 64 + np.arange(64)])
        units.append(("q%d" % jt, cols))
    for kp in range(2):
        units.append(("k%d" % kp, QD + kp * 128 + np.arange(128)))
    for u in range(2):
        units.append(("v%d" % u, QD + KVD + u * 128 + np.arange(128)))
    c0 = QD + 2 * KVD
    for cc in range(NCC):
        units.append(("ca%d" % cc, c0 + cc * 128 + np.arange(128)))
        units.append(("cg%d" % cc, c0 + CD + cc * 128 + np.arange(128)))
    g0 = c0 + 2 * CD
    for j in range(KC):
        units.append(("ga%d" % j, g0 + j * 128 + np.arange(128)))
        units.append(("gb%d" % j, g0 + D + j * 128 + np.arange(128)))
    return units


def ot_rows():
    rows = []
    for hk in range(NKV):
        for c in range(4):
            hlo = 8 * hk + c
            hhi = 8 * hk + 4 + c
            rows.append(hlo * 64 + np.arange(64))
            rows.append(hhi * 64 + np.arange(64))
    return np.concatenate(rows)


def unit_schedule():
    sched = []
    iu = in_units()
    for name, _ in iu[:16 + 2 + 2 + 2 * NCC]:
        sched.append((name, 4096))
    for j in range(KC):
        sched.append(("ga%d" % j, 4096))
        sched.append(("gb%d" % j, 4096))
        sched.append(("ao%d" % j, 2048))
        sched.append(("co%d" % j, 2048))
    for j in range(KC):
        sched.append(("wo%d" % j, 4096))
    f0 = 0
    for fg, n in enumerate(FG_SIZES):
        for ft in range(n):
            sched.append(("fgate%d" % (f0 + ft), 4096))
            sched.append(("fup%d" % (f0 + ft), 4096))
        for j in range(KC):
            sched.append(("wd%d_%d" % (fg, j), n * 128))
        f0 += n
    return sched


def _tile_units(w, kchunks):
    K, N = w.shape
    return np.ascontiguousarray(
        w.reshape(kchunks, 128, N // 128, 128).transpose(2, 1, 0, 3)).reshape(N // 128, 128, kchunks * 128)


def pack_weights(w_in, w_ao, w_co, w_out, w_gu, w_down):
    sched = unit_schedule()
    tot = sum(n for _, n in sched)
    wts = np.empty((128, tot), np.float32)
    iu = in_units()
    allcols = np.concatenate([c for _, c in iu])
    win_p = _tile_units(np.ascontiguousarray(w_in[:, allcols]), KC)
    in_idx = {name: i for i, (name, _) in enumerate(iu)}
    ao_u = _tile_units(np.ascontiguousarray(w_ao[ot_rows(), :]), 16)
    co_u = _tile_units(w_co, 16)
    wo_u = _tile_units(w_out, KC)
    gu_u = _tile_units(w_gu, KC)
    off = 0
    f0s = np.cumsum([0] + FG_SIZES)
    for name, n in sched:
        if name in in_idx:
            blk = win_p[in_idx[name]]
        elif name.startswith("ao"):
            blk = ao_u[int(name[2:])]
        elif name.startswith("co"):
            blk = co_u[int(name[2:])]
        elif name.startswith("wo"):
            blk = wo_u[int(name[2:])]
        elif name.startswith("fgate"):
            blk = gu_u[int(name[5:])]
        elif name.startswith("fup"):
            blk = gu_u[NFT + int(name[3:])]
        elif name.startswith("wd"):
            fg, j = name[2:].split("_")
            fg, j = int(fg), int(j)
            f0, nf = f0s[fg], FG_SIZES[fg]
            sub = w_down[f0 * 128:(f0 + nf) * 128, j * 128:(j + 1) * 128]
            blk = sub.reshape(nf, 128, 128).transpose(1, 0, 2).reshape(128, nf * 128)
        else:
            raise KeyError(name)
        wts[:, off:off + n] = blk
        off += n
    assert off == tot
    return wts


def pack_consts(b_in, b_co, g1, g2, gf, conv_w, conv_b, ln_g, ln_b, sinks, b_v_dummy=None):
    cst = np.zeros((128, NCST), np.float32)
    iu = in_units()
    for i, (name, cols) in enumerate(iu):
        cst[:, C_BIN + i] = b_in[cols]
    cst[:, C_BCO:C_BCO + 32] = b_co.reshape(32, 128).T
    cst[:, C_G1:C_G1 + 32] = g1.reshape(32, 128).T
    cst[:, C_G2:C_G2 + 32] = g2.reshape(32, 128).T
    cst[:, C_GF:C_GF + 32] = gf.reshape(32, 128).T
    cw = conv_w.reshape(CW, CD)
    cst[:, C_CW:C_CW + NCC * CW] = cw.reshape(CW, NCC, 128).transpose(2, 1, 0).reshape(128, NCC * CW)
    cst[:, C_CB:C_CB + NCC] = conv_b.reshape(NCC, 128).T
    cst[:, C_LG:C_LG + NCC] = ln_g.reshape(NCC, 128).T
    cst[:, C_LB:C_LB + NCC] = ln_b.reshape(NCC, 128).T
    for hk in range(NKV):
        for c in range(4):
            cst[0:64, C_SK + hk * 4 + c] = sinks[8 * hk + c]
            cst[64:128, C_SK + hk * 4 + c] = sinks[8 * hk + 4 + c]
    perm = np.zeros((128, 128), np.float32)
    for m in range(128):
        partner = m + 32 if (m % 64) < 32 else m - 32
        perm[partner, m] = 1.0
    cst[:, C_PERM:C_PERM + 128] = perm
    cst[:, C_ID:C_ID + 128] = np.eye(128, dtype=np.float32)
    return cst


def rope_tables(pos):
    inv_freq = (np.float32(10000.0) ** (-np.arange(0, HD, 2, dtype=np.float32) / np.float32(HD))).astype(np.float32)
    ang = (pos[:, None].astype(np.float32) * inv_freq[None, :]).astype(np.float32)
    cos = np.cos(ang).astype(np.float32)
    sin = np.sin(ang).astype(np.float32)
    m = np.arange(128)
    f = m % 32
    sign = np.where((m % 64) < 32, -1.0, 1.0).astype(np.float32)
    cosT = np.ascontiguousarray(cos[:, f].T)
    sinT = np.ascontiguousarray(sin[:, f].T * sign[:, None])
    return cosT, sinT


def core_inputs(core, x, meta):
    b, half = core // 2, core % 2
    xT = np.empty((NG, 128, KC, TS), np.float32)
    tabs = np.empty((NG, 128, 2, TS), np.float32)
    masks = np.empty((NG, 128, 3 * 512), np.float32)
    cmask = np.ones((NG, 128, 128), np.float32)
    jj = np.arange(128)[:, None]
    ii = np.arange(128)[None, :]
    mprev = (jj > ii).astype(np.float32)
    mcur = (jj <= ii).astype(np.float32)
    for g in range(NG):
        s0 = half * 2048 + g * T
        slab = np.zeros((TS, D), np.float32)
        pos = np.zeros((TS,), np.float32)
        if s0 >= HALO:
            slab[0:HALO] = x[b, s0 - HALO:s0]
            pos[0:HALO] = N_META + np.arange(s0 - HALO, s0)
            m0 = mprev
        else:
            slab[HALO - N_META:HALO] = meta
            pos[HALO - N_META:HALO] = np.arange(N_META)
            m0 = np.zeros_like(mprev)
            cmask[g, :, 0:HALO - N_META] = 0.0
        slab[HALO:HALO + T] = x[b, s0:s0 + T]
        pos[HALO:HALO + T] = N_META + np.arange(s0, s0 + T)
        slab[HALO + T:] = meta
        pos[HALO + T:] = np.arange(N_META)
        xT[g] = slab.T.reshape(KC, 128, TS).transpose(1, 0, 2)
        c, s = rope_tables(pos)
        tabs[g, :, 0, :] = c
        tabs[g, :, 1, :] = s
        masks[g, :, 0:512] = np.tile(m0, (1, 4))
        masks[g, :, 512:1024] = np.tile(mprev, (1, 4))
        masks[g, :, 1024:1536] = np.tile(mcur, (1, 4))
    return {"xT": xT, "tabs": tabs, "masks": masks, "cmask": cmask}


class Tok:
    __slots__ = ("sem", "val", "eng", "key")

    def __init__(self, sem, val, eng, key):
        self.sem, self.val, self.eng, self.key = sem, val, eng, key


class Eng:
    def __init__(self, name, sem, is_pe=False):
        self.name, self.sem, self.is_pe = name, sem, is_pe
        self.n = 0
        self.known = {}
        self.prog = []

    def record(self, fn, deps, inc=True):
        waits = []
        for d in deps:
            if d is None:
                continue
            if self.known.get(d.key, 0) >= d.val:
                continue
            self.known[d.key] = d.val
            waits.append((d.sem, d.val))
        tok = None
        if inc:
            self.n += 1
            tok = Tok(self.sem, self.n, self, "E" + self.name)
        self.prog.append((waits, fn, inc))
        return tok

    def replay(self, h):
        for waits, fn, inc in self.prog:
            for s, v in waits:
                h.wait_ge(s, v)
            if fn is None:
                continue
            ins = fn(h)
            if inc:
                ins.then_inc(self.sem, 1)


class Res:
    __slots__ = ("w", "r")

    def __init__(self):
        self.w = None
        self.r = {}


def res_tokens(rs):
    out = []
    for r in rs:
        if r.w is not None:
            out.append(r.w)
        out.extend(r.r.values())
    return out


def op(eng, fn, reads=(), writes=(), extra=(), inc=True, tok_override=None):
    deps = []
    for r in reads:
        if r.w is not None:
            deps.append(("raw", r.w))
    for w in writes:
        if w.w is not None:
            deps.append(("waw", w.w))
        for t in w.r.values():
            deps.append(("war", t))
    for t in extra:
        deps.append(("raw", t))
    fl = []
    for kind, t in deps:
        if t.eng is eng:
            if eng.is_pe:
                continue
            if not (kind == "raw" and SAME_ENGINE_SYNC):
                continue
        fl.append(t)
    tok = eng.record(fn, fl, inc)
    if tok_override is not None:
        tok = tok_override
    if tok is not None:
        for r in reads:
            r.r[tok.key] = tok
        for w in writes:
            w.w = tok
            w.r = {}
    return tok


class DmaSlot:
    def __init__(self, sem, name):
        self.sem, self.name, self.cnt = sem, name, 0
        self.res = Res()

    def next_tok(self):
        self.cnt += 1
        return Tok(self.sem, 16 * self.cnt, None, "D" + self.name)


def dma(queue, slot, out_ap, in_ap, reads=(), writes=(), extra=()):
    tok = slot.next_tok()
    sem = slot.sem

    def fn(h):
        return h.dma_start(out=out_ap, in_=in_ap).then_inc(sem, 16)

    op(queue, fn, reads=reads, writes=writes, extra=extra, inc=False, tok_override=tok)
    return tok


def build_nc(ng=NG, debug=False):
    nc = bass.Bass("TRN2", target_bir_lowering=False)
    sched = unit_schedule()
    tot_cols = sum(n for _, n in sched)
    xT_d = nc.dram_tensor("xT", [ng, 128, KC, TS], F32, kind="ExternalInput").ap()
    tabs_d = nc.dram_tensor("tabs", [ng, 128, 2, TS], F32, kind="ExternalInput").ap()
    masks_d = nc.dram_tensor("masks", [ng, 128, 1536], F32, kind="ExternalInput").ap()
    cmask_d = nc.dram_tensor("cmask", [ng, 128, 128], F32, kind="ExternalInput").ap()
    cst_d = nc.dram_tensor("cst", [128, NCST], F32, kind="ExternalInput").ap()
    bv_d = nc.dram_tensor("bv", [128, 256], F32, kind="ExternalInput").ap()
    wts_d = nc.dram_tensor("wts", [128, tot_cols], F32, kind="ExternalInput").ap()
    out_d = nc.dram_tensor("outT", [ng, D, T], F32, kind="ExternalOutput").ap()
    dbg = {}
    if debug:
        for nm, shp, dt_ in [("d_u", [128, KC * TS], BF16), ("d_q", [128, 16 * T], BF16), ("d_k", [128, 2 * TS], BF16),
                             ("d_ot", [128, 16 * T], BF16), ("d_cs", [128, 16 * T], BF16), ("d_mg", [128, 32 * T], BF16),
                             ("d_h1", [128, 32 * T], F32), ("d_y", [128, 16 * T], F32)]:
            dbg[nm] = nc.dram_tensor(nm, shp, dt_, kind="ExternalOutput").ap()

    from contextlib import ExitStack
    with ExitStack() as es:
        def sb(name, shape, dt):
            return es.enter_context(nc.sbuf_tensor(name, shape, dt))

        def sem(name):
            return es.enter_context(nc.semaphore(name))

        R1 = sb("R1", [128, 16384], F32)
        R2 = sb("R2", [128, 8192], F32)
        U = sb("U", [128, KC, TS], BF16)
        WS = [sb("W%d" % i, [128, 4096], BF16) for i in range(4)]
        XR = [sb("XR%d" % i, [128, TS], F32) for i in range(3)]
        SQ = [sb("SQ%d" % i, [128, TS], BF16) for i in range(2)]
        RSTD = sb("RSTD", [128, TS], F32)
        ST = [sb("ST%d" % i, [128, T], F32) for i in range(3)]
        MSK = sb("MSK", [128, 1536], BF16)
        CMK = sb("CMK", [128, 128], F32)
        CST = sb("CST", [128, NCST], F32)
        BV = sb("BV", [128, 256], F32)
        ONEB = sb("ONEB", [128, 128], BF16)
        ONEF = sb("ONEF", [128, 128], F32)
        ONELH = sb("ONELH", [128, 2, 128], BF16)
        ESK = sb("ESK", [128, 16], F32)
        IDB = sb("IDB", [128, 128], BF16)
        FT = [sb("FT%d" % i, [128, T], F32) for i in range(4)]
        banks = [es.enter_context(nc.psum_tensor("PS%d" % i, [128, 512], F32)) for i in range(8)]

        H1 = R1[:, :].rearrange("p (a b) -> p a b", b=T)
        QT = R1[:, 0:4096].bitcast(BF16).rearrange("p (a b) -> p a b", b=T)
        CSB = QT
        OT = R1[:, 4096:8192].bitcast(BF16).rearrange("p (a b) -> p a b", b=T)
        KT = R1[:, 8192:8192 + TS].bitcast(BF16).rearrange("p (a b) -> p a b", b=TS)
        VA = R1[:, 8848:8848 + 3072].bitcast(BF16).rearrange("p (t h v c) -> p t h v c", t=6, h=4, v=2)
        VA_flat = R1[:, 8848:8848 + 3072].bitcast(BF16)
        CB = [R1[:, 11920 + i * 320:11920 + (i + 1) * 320].bitcast(BF16) for i in range(2)]
        NDG = 6
        DG = [R1[:, 15792 + i * 64:15792 + (i + 1) * 64].bitcast(BF16) for i in range(NDG)]
        SG = [R1[:, 13200 + i * 640:13200 + (i + 1) * 640] for i in range(2)]
        QF = [R1[:, 14480 + i * TS:14480 + (i + 1) * TS] for i in range(2)]
        TAB = R2[:, 0:2 * TS].rearrange("p (a b) -> p a b", b=TS)
        RT = [R2[:, 1312 + i * TS:1312 + (i + 1) * TS] for i in range(2)]
        ET = [R2[:, 4096 + i * 256:4096 + (i + 1) * 256].bitcast(BF16) for i in range(4)]
        PT = [R2[:, 5120 + i * 256:5120 + (i + 1) * 256].bitcast(BF16) for i in range(8)]
        PM = [R2[:, 7168 + i * 256:7168 + (i + 1) * 256].bitcast(BF16) for i in range(4)]
        Y = R2[:, :].rearrange("p (a b) -> p a b", b=T)
        MG = R2[:, :].bitcast(BF16).rearrange("p (a b) -> p a b", b=T)
        ACT = [R2[:, i * 2816:(i + 1) * 2816].bitcast(BF16).rearrange("p (a b) -> p a b", b=T) for i in range(2)]

        PE = Eng("pe", sem("s_pe"), is_pe=True)
        AC = Eng("act", sem("s_act"))
        DV = Eng("dve", sem("s_dve"))
        PO = Eng("pool", sem("s_pool"))
        SP = Eng("sp", sem("s_sp"))
        wslots = [DmaSlot(sem("s_w%d" % i), "w%d" % i) for i in range(4)]
        xslots = [DmaSlot(sem("s_x%d" % i), "x%d" % i) for i in range(3)]
        cslot = DmaSlot(sem("s_c"), "c")
        tslot = DmaSlot(sem("s_t"), "t")
        mslot = DmaSlot(sem("s_m"), "m")
        oslots = [DmaSlot(sem("s_o%d" % i), "o%d" % i) for i in range(4)]
        dslots = [DmaSlot(sem("s_d%d" % i), "d%d" % i) for i in range(8)] if debug else []
        dstate = {"i": 0}

        def next_dslot():
            d = dslots[dstate["i"]]
            dstate["i"] += 1
            return d

        r_u = Res(); r_qcs = Res(); r_ot = Res(); r_kt = Res(); r_va = Res()
        r_dg = [Res() for _ in range(6)]
        r_cb = [Res(), Res()]; r_sg = [Res(), Res()]; r_qf = [Res(), Res()]
        r_rt = [Res(), Res()]; r_et = [Res() for _ in range(4)]; r_pt = [Res() for _ in range(8)]
        r_pm = [Res() for _ in range(4)]
        r_y = [Res() for _ in range(NCC)]
        r_mg = Res()
        r_act = [Res(), Res()]
        r_h1 = [Res() for _ in range(KC)]
        r_sq = [Res(), Res()]
        r_rstd = Res()
        r_st = [Res() for _ in range(3)]
        r_ft = [Res() for _ in range(4)]
        r_bank = [Res() for _ in range(8)]
        r_cst = Res(); r_const = Res()
        r_tab = Res(); r_msk = Res()
        state = {"bank": 0, "ft": 0, "x": 0, "sq": 0, "w": 0, "o": 0}

        pinned = set()

        def next_bank():
            b = state["bank"]
            while b in pinned:
                b = (b + 1) % 6
            state["bank"] = (b + 1) % 6
            return banks[b], r_bank[b]

        def bank_index(bk_res):
            return r_bank.index(bk_res)

        def next_ft():
            i = state["ft"]
            state["ft"] = (i + 1) % 4
            return FT[i], r_ft[i]

        woffs = np.cumsum([0] + [n for _, n in sched])
        wstate = {"issued": 0, "used": 0, "goff": 0}
        nunits = len(sched)
        total_units = nunits * ng

        def w_issue_upto(k):
            while wstate["issued"] < min(k, total_units):
                i = wstate["issued"]
                u = i % nunits
                slot = wslots[i % 4]
                n = sched[u][1]
                dma(PO, slot, WS[i % 4][:, 0:n], wts_d[:, int(woffs[u]):int(woffs[u]) + n],
                    writes=[slot.res])
                wstate["issued"] += 1

        def w_next(name):
            i = wstate["used"]
            u = i % nunits
            assert sched[u][0] == name, (sched[u][0], name)
            w_issue_upto(i + 3)
            wstate["used"] += 1
            return WS[i % 4], wslots[i % 4].res, i

        def w_done():
            w_issue_upto(wstate["used"] + 3)

        def mm_chain(out_ap, pairs, reads, bres, extra=()):
            n = len(pairs)

            def fn(h):
                ins = None
                for i, (l, r) in enumerate(pairs):
                    ins = h.matmul(out_ap, l, r, start=(i == 0), stop=(i == n - 1))
                return ins
            return op(PE, fn, reads=reads, writes=[bres], extra=extra)

        def act(out_ap, in_ap, func, reads, writes, bias=None, scale=None, extra=()):
            kw = {}
            if bias is not None:
                kw["bias"] = bias
            if scale is not None:
                kw["scale"] = scale
            return op(AC, lambda h: h.activation(out_ap, in_ap, func, **kw), reads=reads, writes=writes, extra=extra)

        def tt(out_ap, a, b, alu, reads, writes, extra=()):
            return op(DV, lambda h: h.tensor_tensor(out_ap, a, b, alu), reads=reads, writes=writes, extra=extra)

        def stt(out_ap, in0, scalar, in1, op0, op1, reads, writes, extra=()):
            return op(DV, lambda h: h.scalar_tensor_tensor(out_ap, in0, scalar, in1, op0, op1),
                      reads=reads, writes=writes, extra=extra)

        def ts(out_ap, in0, s1, s2, op0, op1, reads, writes, extra=()):
            if s2 is None:
                return op(DV, lambda h: h.tensor_scalar(out_ap, in0, s1, None, op0), reads=reads, writes=writes, extra=extra)
            return op(DV, lambda h: h.tensor_scalar(out_ap, in0, s1, s2, op0, op1), reads=reads, writes=writes, extra=extra)

        def cc_(c0, n=1):
            return CST[:, c0:c0 + n]

        dma(SP, cslot, CST[:, :], cst_d[:, :], writes=[r_cst])
        dma(SP, cslot, BV[:, :], bv_d[:, :], writes=[r_cst])
        op(DV, lambda h: h.memset(ONEB[:, :], 1.0), writes=[r_const])
        op(DV, lambda h: h.memset(ONEF[:, :], 1.0), writes=[r_const])
        op(DV, lambda h: h.memset(ONELH[:, :, :], 0.0), writes=[r_const])
        op(DV, lambda h: h.memset(ONELH[:, 0, 0:64], 1.0), writes=[r_const])
        op(DV, lambda h: h.memset(ONELH[:, 1, 64:128], 1.0), writes=[r_const])
        act(ESK[:, :], cc_(C_SK, 16), AF.Exp, reads=[r_cst], writes=[r_const])
        op(DV, lambda h: h.tensor_copy(IDB[:, :], CST[:, C_ID:C_ID + 128]), reads=[r_cst], writes=[r_const])


        EPSB = sb("EPSB", [128, 1], F32)
        CARRY = sb("CARRY", [128, NCC, 32], BF16)
        r_carry = [Res() for _ in range(NCC)]
        if debug:
            print("sbuf bytes remaining", nc.sbuf_bytes_remaining)
        op(DV, lambda h: h.memset(EPSB[:, :], EPS), writes=[r_const])

        def mm1(out_ap, l, r, st, sp_):
            return lambda h: h.matmul(out_ap, l, r, start=st, stop=sp_)

        def recip(ap, reads, writes):
            return op(DV, lambda h: h.reciprocal(ap, ap), reads=reads, writes=writes)

        def xload(src_ap, n):
            xi = state["x"]
            state["x"] = (xi + 1) % 3
            dma(SP, xslots[xi], XR[xi][:, 0:n], src_ap, writes=[xslots[xi].res])
            return XR[xi], xslots[xi].res

        def next_sq():
            si = state["sq"]
            state["sq"] ^= 1
            return SQ[si], r_sq[si]

        def wtile(W, i):
            return W[:, i * 128:(i + 1) * 128]

        fn_q = []
        fn_done = set()

        def emit_fn(n):
            for _ in range(min(n, len(fn_q))):
                fn_q.pop(0)()

        def stats_chunk(gg, kc):
            xr, xres = xload(xT_d[gg, :, kc, 128:640], 512)
            sq, sqr = next_sq()
            act(sq[:, 0:512], xr[:, 0:512], AF.Square, reads=[xres], writes=[sqr])
            op(PE, mm1(banks[6][:, :], ONEB[:, :], sq[:, 0:512], kc == 0, kc == KC - 1),
               reads=[sqr, r_const], writes=[r_bank[6]])

        def stats_finish_own():
            act(RSTD[:, 128:640], banks[6][:, :], AF.Ln, reads=[r_bank[6], r_const], writes=[r_rstd],
                bias=EPSB[:, 0:1], scale=1.0 / D)
            act(RSTD[:, 128:640], RSTD[:, 128:640], AF.Exp, reads=[r_rstd], writes=[r_rstd], scale=-0.5)

        def stats_finish():
            act(RSTD[:, 0:512], banks[6][:, :], AF.Ln, reads=[r_bank[6], r_const], writes=[r_rstd],
                bias=EPSB[:, 0:1], scale=1.0 / D)
            act(RSTD[:, 512:656], banks[7][:, 0:144], AF.Ln, reads=[r_bank[7], r_const], writes=[r_rstd],
                bias=EPSB[:, 0:1], scale=1.0 / D)
            act(RSTD[:, :], RSTD[:, :], AF.Exp, reads=[r_rstd], writes=[r_rstd], scale=-0.5)

        def u_chunk(gg, kc):
            op(DV, lambda h, kc=kc: h.tensor_copy(U[:, kc, 0:128], U[:, kc, 512:640]), reads=[r_u], writes=[r_u])
            xr, xres = xload(xT_d[gg, :, kc, 128:640], 512)
            stt(U[:, kc, 128:640], xr[:, 0:512], cc_(C_G1 + kc), RSTD[:, 128:640], ALU.mult, ALU.mult,
                reads=[xres, r_rstd, r_cst], writes=[r_u])

        for g in range(ng):
            emit_fn(6)
            pro_tokens = []
            if g == 0:
                pslots = [DmaSlot(sem("s_p%d" % q), "p%d" % q) for q in range(4)]
                xviews = []
                for q in range(4):
                    base = R1[:, q * 8 * TS:(q + 1) * 8 * TS] if q < 3 else R2[:, 0:8 * TS]
                    v = base.rearrange("p (a b) -> p a b", b=TS)
                    dma(SP, pslots[q], v, xT_d[0, :, 8 * q:8 * q + 8, :], writes=[pslots[q].res])
                    xviews.extend((v[:, i, :], pslots[q].res) for i in range(8))
                for kc in range(KC):
                    xv, xvr = xviews[kc]
                    sq, sqr = next_sq()
                    act(sq[:, :], xv, AF.Square, reads=[xvr], writes=[sqr])
                    op(PE, mm1(banks[6][:, :], ONEB[:, :], sq[:, 0:512], kc == 0, kc == KC - 1),
                       reads=[sqr, r_const], writes=[r_bank[6]])
                    op(PE, mm1(banks[7][:, 0:144], ONEB[:, :], sq[:, 512:656], kc == 0, kc == KC - 1),
                       reads=[sqr, r_const], writes=[r_bank[7]])
                stats_finish()
                for kc in range(KC):
                    xv, xvr = xviews[kc]
                    stt(U[:, kc, :], xv, cc_(C_G1 + kc), RSTD[:, :], ALU.mult, ALU.mult,
                        reads=[xvr, r_rstd, r_cst], writes=[r_u])
                pro_tokens = res_tokens([p_.res for p_ in pslots])
            fenceR1 = res_tokens(r_h1) + pro_tokens
            fenceQF = res_tokens(r_h1[28:31]) + pro_tokens
            fenceR2 = res_tokens(r_act) + res_tokens([r_mg]) + pro_tokens
            dma(SP, tslot, TAB[:, :, :], tabs_d[g], writes=[r_tab], extra=fenceR2)
            dma(SP, tslot, CMK[:, :], cmask_d[g], writes=[r_tab])
            dma(PO, mslot, MSK[:, :], masks_d[g], writes=[r_msk])
            if debug and g == 0:
                dma(SP, next_dslot(), dbg["d_u"], U[:, :, :].rearrange("p a b -> p (a b)"), reads=[r_u])

            def rope_finish(dst_ap, dres, qf, qfr, segs, tok_lo, tok_hi, dfence=None):
                outs = []
                for (lo, hi) in segs:
                    b2, b2r = next_bank()
                    op(PE, mm1(b2[:, 0:hi - lo], CST[:, C_PERM:C_PERM + 128], qf[:, lo:hi], True, True),
                       reads=[qfr, r_cst], writes=[b2r])
                    outs.append((b2, b2r, lo, hi))
                n = tok_hi - tok_lo
                tt(RT[0][:, 0:n], qf[:, 0:n], TAB[:, 0, tok_lo:tok_hi], ALU.mult,
                   reads=[qfr, r_tab], writes=[r_rt[0]], extra=fenceR2)
                for (b2, b2r, lo, hi) in outs:
                    tt(RT[1][:, lo:hi], b2[:, 0:hi - lo], TAB[:, 1, tok_lo + lo:tok_lo + hi], ALU.mult,
                       reads=[b2r, r_tab], writes=[r_rt[1]], extra=fenceR2)
                tt(dst_ap, RT[0][:, 0:n], RT[1][:, 0:n], ALU.add, reads=[r_rt[0], r_rt[1]], writes=[dres],
                   extra=fenceR1 if dfence is None else dfence)

            def rope_q(jt, qi):
                assert g == 0 or not fn_q or (jt // 2) in fn_done, (jt, sorted(fn_done))
                rope_finish(QT[:, jt, :], r_qcs, QF[qi], r_qf[qi], [(0, 512)], 128, 640, res_tokens([r_h1[jt // 2]]) + pro_tokens)

            pend = None
            for jt in range(16):
                W, wres, _ = w_next("q%d" % jt)
                bk, br = next_bank()
                mm_chain(bk[:, :], [(wtile(W, kc), U[:, kc, 128:640]) for kc in range(KC)],
                         reads=[wres, r_u], bres=br)
                w_done()
                qi = jt % 2
                act(QF[qi][:, 0:512], bk[:, :], AF.Identity, reads=[br, r_cst], writes=[r_qf[qi]],
                    bias=cc_(C_BIN + jt), extra=fenceQF)
                if pend is not None:
                    rope_q(*pend)
                pend = (jt, qi)
                emit_fn(2)
            rope_q(*pend)
            emit_fn(len(fn_q))
            fenceR1 = res_tokens(r_h1) + pro_tokens
            for kp in range(2):
                W, wres, _ = w_next("k%d" % kp)
                bA, brA = next_bank()
                bB, brB = next_bank()
                mm_chain(bA[:, :], [(wtile(W, kc), U[:, kc, 0:512]) for kc in range(KC)], reads=[wres, r_u], bres=brA)
                mm_chain(bB[:, 0:144], [(wtile(W, kc), U[:, kc, 512:656]) for kc in range(KC)], reads=[wres, r_u], bres=brB)
                w_done()
                qi = kp % 2
                act(QF[qi][:, 0:512], bA[:, :], AF.Identity, reads=[brA, r_cst], writes=[r_qf[qi]],
                    bias=cc_(C_BIN + 16 + kp), extra=fenceR1)
                act(QF[qi][:, 512:656], bB[:, 0:144], AF.Identity, reads=[brB, r_cst], writes=[r_qf[qi]],
                    bias=cc_(C_BIN + 16 + kp), extra=fenceR1)
                rope_finish(KT[:, kp, :], r_kt, QF[qi], r_qf[qi], [(0, 512), (512, 656)], 0, 656)

            op(DV, lambda h: h.memset(VA_flat[:, :], 0.0), writes=[r_va], extra=fenceR1)
            for u in range(2):
                W, wres, _ = w_next("v%d" % u)
                bA, brA = next_bank()
                bB, brB = next_bank()
                outs = []
                for tb in range(6):
                    M = 128 if tb < 5 else 16
                    t0 = tb * 128
                    if tb < 4:
                        o_ap, o_r = bA[0:M, tb * 128:(tb + 1) * 128], brA
                    else:
                        o_ap, o_r = bB[0:M, (tb - 4) * 128:(tb - 3) * 128], brB
                    mm_chain(o_ap, [(U[:, kc, t0:t0 + M], wtile(W, kc)) for kc in range(KC)],
                             reads=[wres, r_u], bres=o_r)
                    outs.append((o_ap, o_r, M))
                w_done()
                for tb, (o_ap, o_r, M) in enumerate(outs):
                    for hh in range(2):
                        hk = 2 * u + hh
                        tt(VA[0:M, tb, hk, 0, 0:64], o_ap[:, hh * 64:(hh + 1) * 64], BV[0:M, hk * 64:(hk + 1) * 64],
                           ALU.add, reads=[o_r, r_cst], writes=[r_va])
                        tt(VA[0:M, tb, hk, 1, 64:128], o_ap[:, hh * 64:(hh + 1) * 64], BV[0:M, hk * 64:(hk + 1) * 64],
                           ALU.add, reads=[o_r, r_cst], writes=[r_va])

            def att_scores(blk, hk):
                P0 = (hk % 2) * 64
                kp = hk // 2
                tbase = kp * 8
                keyspec = [(KT[P0:P0 + 64, kp, blk * 128:(blk + 1) * 128], 128, blk, 0 if blk == 0 else 1),
                           (KT[P0:P0 + 64, kp, (blk + 1) * 128:(blk + 2) * 128], 128, blk + 1, 2),
                           (KT[P0:P0 + 64, kp, 640:656], 16, 5, None)]
                ptiles = []
                par = (blk * 4 + hk) % 2
                for ki, (kap, M, tbi, mi) in enumerate(keyspec):
                    for half in range(2):
                        bk, br = next_bank()
                        rhs = QT[P0:P0 + 64, tbase + half * 4:tbase + half * 4 + 4, blk * 128:(blk + 1) * 128]
                        o3 = bk[0:M, :].rearrange("p (a b) -> p a b", a=4)
                        mm_chain(o3, [(kap, rhs)], reads=[r_kt, r_qcs], bres=br)
                        if mi is not None:
                            ei = ki * 2 + half
                            act(ET[ei][:, :], bk[:, :], AF.Exp, reads=[br], writes=[r_et[ei]], scale=0.125,
                                extra=fenceR2)
                            pi = ei + 4 * par
                            tt(PT[pi][:, :], ET[ei][:, :], MSK[:, mi * 512:(mi + 1) * 512], ALU.mult,
                               reads=[r_et[ei], r_msk], writes=[r_pt[pi]], extra=fenceR2)
                            ptiles.append((PT[pi][:, :], r_pt[pi], 128, tbi, half))
                        else:
                            mi_ = half + 2 * par
                            act(PM[mi_][0:16, :], bk[0:16, :], AF.Exp, reads=[br], writes=[r_pm[mi_]], scale=0.125,
                                extra=fenceR2)
                            ptiles.append((PM[mi_][0:16, :], r_pm[mi_], 16, 5, half))
                return ptiles

            def att_pv(blk, hk, ptiles):
                bO, brO = next_bank()
                bD, brD = next_bank()
                for i, (pap, pres, M, tbi, half) in enumerate(ptiles):
                    op(PE, mm1(bO[:, :], VA[0:M, tbi, hk, half, :], pap, i == 0, i == 5),
                       reads=[pres, r_va], writes=[brO])
                for i, (pap, pres, M, tbi, half) in enumerate(ptiles):
                    op(PE, mm1(bD[:, :], ONELH[0:M, half, :], pap, i == 0, i == 5),
                       reads=[pres, r_const], writes=[brD])
                f, fr = next_ft()
                for c in range(4):
                    act(f[:, c * 128:(c + 1) * 128], bD[:, c * 128:(c + 1) * 128], AF.Ln, reads=[brD, r_const], writes=[fr],
                        bias=ESK[:, hk * 4 + c:hk * 4 + c + 1])
                act(f[:, :], f[:, :], AF.Exp, reads=[fr], writes=[fr], scale=-1.0)
                tt(OT[:, hk * 4:hk * 4 + 4, blk * 128:(blk + 1) * 128], bO[:, :].rearrange("p (c q) -> p c q", c=4),
                   f[:, :].rearrange("p (c q) -> p c q", c=4), ALU.mult, reads=[brO, fr], writes=[r_ot], extra=fenceR1)

            its = [(blk, hk) for blk in range(4) for hk in range(4)]
            att_state = {"i": 0, "prev": None}

            def att_step():
                i = att_state["i"]
                if i < len(its):
                    blk, hk = its[i]
                    pt_ = att_scores(blk, hk)
                    if att_state["prev"] is not None:
                        att_pv(*att_state["prev"])
                    att_state["prev"] = (blk, hk, pt_)
                    att_state["i"] = i + 1
                elif att_state["prev"] is not None:
                    att_pv(*att_state["prev"])
                    att_state["prev"] = None

            fence_y_lo = res_tokens(r_rt + [r_tab])

            def fence_y_for(cc):
                if cc < 8:
                    return fence_y_lo
                assert att_state["i"] == len(its) and att_state["prev"] is None
                return fence_y_lo + res_tokens(r_et + r_pt + r_pm)

            def conv_pe(cc, ci):
                bk, br = next_bank()
                for k in range(CW):
                    di = dgs["i"]
                    dgs["i"] = (di + 1) % NDG
                    ts(DG[di][:, :], IDB[:, :], cc_(C_CW + cc * CW + k), None, ALU.mult, None,
                       reads=[r_const, r_cst], writes=[r_dg[di]], extra=fenceR1)
                    op(PE, mm1(bk[:, :], DG[di][:, :], CB[ci][:, 98 + k:610 + k], k == 0, k == CW - 1),
                       reads=[r_dg[di], r_cb[ci]], writes=[br])
                act(Y[:, cc, :], bk[:, :], AF.Identity, reads=[br, r_cst], writes=[r_y[cc]], bias=cc_(C_CB + cc),
                    extra=fence_y_for(cc))
                f, fr = next_ft()
                act(f[:, :], Y[:, cc, :], AF.Square, reads=[r_y[cc]], writes=[fr])
                if stq:
                    ln_stat_mm(*stq.pop(0))
                stq.append((cc, f, fr))

            def ln_stat_mm(cc, f, fr):
                op(PE, mm1(banks[6][:, :], ONEF[:, :], Y[:, cc, :], cc == 0, cc == NCC - 1),
                   reads=[r_y[cc], r_const], writes=[r_bank[6]])
                op(PE, mm1(banks[7][:, :], ONEF[:, :], f[:, :], cc == 0, cc == NCC - 1),
                   reads=[fr, r_const], writes=[r_bank[7]])

            stq = []
            dgs = {"i": 0}
            pendc = None
            for cc in range(NCC):
                ci = cc % 2
                ua = C_BIN + 20 + 2 * cc
                ug = ua + 1
                if g == 0:
                    W, wres, _ = w_next("ca%d" % cc)
                    bA0, rA0 = next_bank()
                    bA1, rA1 = next_bank()
                    mm_chain(bA0[:, :], [(wtile(W, kc), U[:, kc, 0:512]) for kc in range(KC)], reads=[wres, r_u], bres=rA0)
                    mm_chain(bA1[:, 0:128], [(wtile(W, kc), U[:, kc, 512:640]) for kc in range(KC)], reads=[wres, r_u], bres=rA1)
                    w_done()
                    pinned.update([bank_index(rA0), bank_index(rA1)])
                    att_step()
                    W, wres, _ = w_next("cg%d" % cc)
                    bG0, rG0 = next_bank()
                    bG1, rG1 = next_bank()
                    mm_chain(bG0[:, :], [(wtile(W, kc), U[:, kc, 0:512]) for kc in range(KC)], reads=[wres, r_u], bres=rG0)
                    mm_chain(bG1[:, 0:128], [(wtile(W, kc), U[:, kc, 512:640]) for kc in range(KC)], reads=[wres, r_u], bres=rG1)
                    w_done()
                    act(SG[ci][:, 0:512], bG0[:, :], AF.Sigmoid, reads=[rG0, r_cst], writes=[r_sg[ci]], bias=cc_(ug), extra=fenceR1)
                    act(SG[ci][:, 512:640], bG1[:, 0:128], AF.Sigmoid, reads=[rG1, r_cst], writes=[r_sg[ci]], bias=cc_(ug), extra=fenceR1)
                    if pendc is not None:
                        conv_pe(*pendc)
                    stt(CB[ci][:, 0:512], bA0[:, :], cc_(ua), SG[ci][:, 0:512], ALU.add, ALU.mult,
                        reads=[rA0, r_sg[ci], r_cst], writes=[r_cb[ci]], extra=fenceR1)
                    stt(CB[ci][:, 512:640], bA1[:, 0:128], cc_(ua), SG[ci][:, 512:640], ALU.add, ALU.mult,
                        reads=[rA1, r_sg[ci], r_cst], writes=[r_cb[ci]], extra=fenceR1)
                    tt(CB[ci][:, 0:128], CB[ci][:, 0:128], CMK[:, :], ALU.mult, reads=[r_cb[ci], r_tab], writes=[r_cb[ci]])
                    pinned.clear()
                else:
                    W, wres, _ = w_next("ca%d" % cc)
                    bA0, rA0 = next_bank()
                    mm_chain(bA0[:, :], [(wtile(W, kc), U[:, kc, 128:640]) for kc in range(KC)], reads=[wres, r_u], bres=rA0)
                    w_done()
                    pinned.add(bank_index(rA0))
                    att_step()
                    W, wres, _ = w_next("cg%d" % cc)
                    bG0, rG0 = next_bank()
                    mm_chain(bG0[:, :], [(wtile(W, kc), U[:, kc, 128:640]) for kc in range(KC)], reads=[wres, r_u], bres=rG0)
                    w_done()
                    act(SG[ci][:, 128:640], bG0[:, :], AF.Sigmoid, reads=[rG0, r_cst], writes=[r_sg[ci]], bias=cc_(ug), extra=fenceR1)
                    if pendc is not None:
                        conv_pe(*pendc)
                    op(DV, lambda h, cc=cc, ci=ci: h.tensor_copy(CB[ci][:, 98:128], CARRY[:, cc, 0:30]),
                       reads=[r_carry[cc]], writes=[r_cb[ci]], extra=fenceR1)
                    stt(CB[ci][:, 128:640], bA0[:, :], cc_(ua), SG[ci][:, 128:640], ALU.add, ALU.mult,
                        reads=[rA0, r_sg[ci], r_cst], writes=[r_cb[ci]], extra=fenceR1)
                    pinned.clear()
                if g + 1 < ng:
                    op(DV, lambda h, cc=cc, ci=ci: h.tensor_copy(CARRY[:, cc, 0:30], CB[ci][:, 610:640]),
                       reads=[r_cb[ci]], writes=[r_carry[cc]])
                pendc = (cc, ci)
                att_step()
            conv_pe(*pendc)
            assert att_state["i"] == len(its) and att_state["prev"] is None
            if debug and g == 0:
                dma(SP, next_dslot(), dbg["d_q"], QT[:, :, :].rearrange("p a b -> p (a b)"), reads=[r_qcs])
                dma(SP, next_dslot(), dbg["d_k"], KT[:, :, :].rearrange("p a b -> p (a b)"), reads=[r_kt])
                dma(SP, next_dslot(), dbg["d_ot"], OT[:, :, :].rearrange("p a b -> p (a b)"), reads=[r_ot])
            while stq:
                ln_stat_mm(*stq.pop(0))
            ts(ST[0][:, :], banks[6][:, :], 1.0 / CD, None, ALU.mult, None, reads=[r_bank[6]], writes=[r_st[0]])
            tt(ST[2][:, :], ST[0][:, :], ST[0][:, :], ALU.mult, reads=[r_st[0]], writes=[r_st[2]])
            stt(ST[2][:, :], banks[7][:, :], 1.0 / CD, ST[2][:, :], ALU.mult, ALU.subtract,
                reads=[r_bank[7], r_st[2]], writes=[r_st[2]])
            act(ST[1][:, :], ST[2][:, :], AF.Ln, reads=[r_st[2], r_const], writes=[r_st[1]], bias=EPSB[:, 0:1], scale=1.0)
            act(ST[1][:, :], ST[1][:, :], AF.Exp, reads=[r_st[1]], writes=[r_st[1]], scale=-0.5)
            if debug and g == 0:
                dma(SP, next_dslot(), dbg["d_y"], Y[:, :, :].rearrange("p a b -> p (a b)"), reads=r_y)
            for cc in range(NCC):
                f, fr = next_ft()
                tt(f[:, :], Y[:, cc, :], ST[0][:, :], ALU.subtract, reads=[r_y[cc], r_st[0]], writes=[fr])
                tt(f[:, :], f[:, :], ST[1][:, :], ALU.mult, reads=[fr, r_st[1]], writes=[fr])
                act(CSB[:, cc, :], f[:, :], AF.Silu, reads=[fr, r_cst], writes=[r_qcs],
                    bias=cc_(C_LB + cc), scale=cc_(C_LG + cc))
            if debug and g == 0:
                dma(SP, next_dslot(), dbg["d_cs"], CSB[:, :, :].rearrange("p a b -> p (a b)"), reads=[r_qcs])

            fence_mg = res_tokens(r_y)
            for j in range(KC):
                W, wres, _ = w_next("ga%d" % j)
                bk, br = next_bank()
                mm_chain(bk[:, :], [(wtile(W, kc), U[:, kc, 128:640]) for kc in range(KC)], reads=[wres, r_u], bres=br)
                w_done()
                fa, fra = next_ft()
                act(fa[:, :], bk[:, :], AF.Sigmoid, reads=[br, r_cst], writes=[fra], bias=cc_(C_BIN + 52 + 2 * j))
                W, wres, _ = w_next("gb%d" % j)
                bk, br = next_bank()
                mm_chain(bk[:, :], [(wtile(W, kc), U[:, kc, 128:640]) for kc in range(KC)], reads=[wres, r_u], bres=br)
                w_done()
                fb, frb = next_ft()
                act(fb[:, :], bk[:, :], AF.Sigmoid, reads=[br, r_cst], writes=[frb], bias=cc_(C_BIN + 53 + 2 * j))
                W, wres, _ = w_next("ao%d" % j)
                bk, br = next_bank()
                mm_chain(bk[:, :], [(wtile(W, oc), OT[:, oc, :]) for oc in range(16)], reads=[wres, r_ot], bres=br)
                w_done()
                tt(fa[:, :], bk[:, :], fa[:, :], ALU.mult, reads=[br, fra], writes=[fra])
                W, wres, _ = w_next("co%d" % j)
                bk, br = next_bank()
                mm_chain(bk[:, :], [(wtile(W, cc), CSB[:, cc, :]) for cc in range(NCC)], reads=[wres, r_qcs], bres=br)
                w_done()
                stt(fb[:, :], bk[:, :], cc_(C_BCO + j), fb[:, :], ALU.add, ALU.mult, reads=[br, frb, r_cst], writes=[frb])
                tt(MG[:, j, :], fa[:, :], fb[:, :], ALU.add, reads=[fra, frb], writes=[r_mg], extra=fence_mg)
            if debug and g == 0:
                dma(SP, next_dslot(), dbg["d_mg"], MG[:, :, :].rearrange("p a b -> p (a b)"), reads=[r_mg])

            fence_h1 = res_tokens([r_qcs, r_ot, r_kt, r_va] + r_cb + r_sg + r_qf + r_dg)
            pend = None
            for j in range(KC):
                W, wres, _ = w_next("wo%d" % j)
                bk, br = next_bank()
                mm_chain(bk[:, :], [(wtile(W, kc), MG[:, kc, :]) for kc in range(KC)], reads=[wres, r_mg], bres=br)
                w_done()
                xr, xres = xload(xT_d[g, :, j, 128:640], 512)
                tt(H1[:, j, :], bk[:, :], xr[:, 0:512], ALU.add, reads=[br, xres], writes=[r_h1[j]], extra=fence_h1)
                sq, sqr = next_sq()
                act(sq[:, 0:512], H1[:, j, :], AF.Square, reads=[r_h1[j]], writes=[sqr])
                if pend is not None:
                    op(PE, mm1(banks[6][:, :], ONEB[:, :], pend[0][:, 0:512], pend[2] == 0, False),
                       reads=[pend[1], r_const], writes=[r_bank[6]])
                pend = (sq, sqr, j)
            op(PE, mm1(banks[6][:, :], ONEB[:, :], pend[0][:, 0:512], False, True), reads=[pend[1], r_const], writes=[r_bank[6]])
            act(ST[0][:, :], banks[6][:, :], AF.Ln, reads=[r_bank[6], r_const], writes=[r_st[0]], bias=EPSB[:, 0:1], scale=1.0 / D)
            act(ST[0][:, :], ST[0][:, :], AF.Exp, reads=[r_st[0]], writes=[r_st[0]], scale=-0.5)
            if debug and g == 0:
                dma(SP, next_dslot(), dbg["d_h1"], H1[:, :, :].rearrange("p a b -> p (a b)"), reads=r_h1)
            for kc in range(KC):
                stt(U[:, kc, 0:512], H1[:, kc, :], cc_(C_G2 + kc), ST[0][:, :], ALU.mult, ALU.mult,
                    reads=[r_h1[kc], r_st[0], r_cst], writes=[r_u])

            fence_act = res_tokens([r_mg])
            f0 = 0
            nxt = g + 1 if g + 1 < ng else None
            sidx = 0
            for fg, nf in enumerate(FG_SIZES):
                ai = fg % 2
                last = (fg == len(FG_SIZES) - 1)
                for t in range(nf):
                    W, wres, _ = w_next("fgate%d" % (f0 + t))
                    bg, brg = next_bank()
                    mm_chain(bg[:, :], [(wtile(W, kc), U[:, kc, 0:512]) for kc in range(KC)], reads=[wres, r_u], bres=brg)
                    w_done()
                    W, wres, _ = w_next("fup%d" % (f0 + t))
                    bu, bru = next_bank()
                    mm_chain(bu[:, :], [(wtile(W, kc), U[:, kc, 0:512]) for kc in range(KC)], reads=[wres, r_u], bres=bru)
                    w_done()
                    f, fr = next_ft()
                    act(f[:, :], bg[:, :], AF.Silu, reads=[brg], writes=[fr])
                    tt(ACT[ai][:, t, :], f[:, :], bu[:, :], ALU.mult, reads=[fr, bru], writes=[r_act[ai]], extra=fence_act)
                    if nxt is not None and sidx < KC:
                        stats_chunk(nxt, sidx)
                        sidx += 1
                        if sidx == KC:
                            stats_finish_own()
                pendq = []
                for j in range(KC):
                    W, wres, _ = w_next("wd%d_%d" % (fg, j))
                    bk, br = next_bank()
                    mm_chain(bk[:, :], [(wtile(W, t), ACT[ai][:, t, :]) for t in range(nf)], reads=[wres, r_act[ai]], bres=br)
                    w_done()
                    tt(H1[:, j, :], bk[:, :], H1[:, j, :], ALU.add, reads=[br, r_h1[j]], writes=[r_h1[j]])
                    if last:
                        sq, sqr = next_sq()
                        act(sq[:, 0:512], H1[:, j, :], AF.Square, reads=[r_h1[j]], writes=[sqr])
                        pendq.append((sq, sqr, j))
                        if len(pendq) > 1:
                            p_ = pendq.pop(0)
                            op(PE, mm1(banks[6][:, :], ONEB[:, :], p_[0][:, 0:512], p_[2] == 0, False),
                               reads=[p_[1], r_const], writes=[r_bank[6]])
                        if nxt is not None:
                            u_chunk(nxt, j)
                if last:
                    while pendq:
                        p_ = pendq.pop(0)
                        op(PE, mm1(banks[6][:, :], ONEB[:, :], p_[0][:, 0:512], p_[2] == 0, len(pendq) == 0),
                           reads=[p_[1], r_const], writes=[r_bank[6]])
                f0 += nf

            act(ST[1][:, :], banks[6][:, :], AF.Ln, reads=[r_bank[6], r_const], writes=[r_st[1]], bias=EPSB[:, 0:1], scale=1.0 / D)
            act(ST[1][:, :], ST[1][:, :], AF.Exp, reads=[r_st[1]], writes=[r_st[1]], scale=-0.5)
            def fn_op(g_, j):
                def run():
                    i = state["ft"]
                    f, fr = next_ft()
                    stt(f[:, :], H1[:, j, :], cc_(C_GF + j), ST[1][:, :], ALU.mult, ALU.mult,
                        reads=[r_h1[j], r_st[1], r_cst], writes=[fr])
                    dma(SP, oslots[i], out_d[g_, j * 128:(j + 1) * 128, :], f[:, :], reads=[fr])
                    fn_done.add(j)
                return run
            fn_q.extend(fn_op(g, j) for j in list(range(28, KC)) + list(range(28)))
            fn_done.clear()
            if g == ng - 1:
                emit_fn(len(fn_q))

        final = [Tok(s.sem, 16 * s.cnt, None, "D" + s.name) for s in oslots + dslots if s.cnt > 0]
        SP.record(None, final, inc=False)

        with nc.Block() as block:
            @block.tensor
            def _(h):
                PE.replay(h)

            @block.scalar
            def _(h):
                AC.replay(h)

            @block.vector
            def _(h):
                DV.replay(h)

            @block.gpsimd
            def _(h):
                PO.replay(h)

            @block.sync
            def _(h):
                SP.replay(h)
    return nc


_NC_CACHE = {}


def kernel(x, meta_tokens, mix_norm_g, w_in, b_in, attn_sinks, conv_w, conv_b, conv_ln_g, conv_ln_b,
           w_attn_o, w_conv_o, b_conv_o, w_out, ffn_norm_g, w_gate_up, w_down, final_norm_g):
    f = lambda a: np.asarray(a, dtype=np.float32)
    x = f(x); meta = f(meta_tokens)
    wts = pack_weights(f(w_in)[0], f(w_attn_o)[0], f(w_conv_o)[0], f(w_out)[0], f(w_gate_up)[0], f(w_down)[0])
    cst = pack_consts(f(b_in)[0], f(b_conv_o)[0], f(mix_norm_g)[0], f(ffn_norm_g)[0], f(final_norm_g),
                      f(conv_w)[0], f(conv_b)[0], f(conv_ln_g)[0], f(conv_ln_b)[0], f(attn_sinks)[0])
    bv = np.ascontiguousarray(np.broadcast_to(f(b_in)[0][QD + KVD:QD + 2 * KVD][None, :], (128, 256)))
    if "nc" not in _NC_CACHE:
        _NC_CACHE["nc"] = build_nc()
    nc = _NC_CACHE["nc"]
    in_maps = []
    for core in range(NCORES):
        m = core_inputs(core, x, meta)
        m.update({"cst": cst, "bv": bv, "wts": wts})
        in_maps.append(m)
    res = run_bass_kernel_spmd(nc, in_maps, core_ids=list(range(NCORES)))
    out = np.empty((BATCH, SEQ, D), np.float32)
    for core in range(NCORES):
        b, half = core // 2, core % 2
        o = np.asarray(res.results[core]["outT"])
        for g in range(NG):
            s0 = half * 2048 + g * T
            out[b, s0:s0 + T, :] = o[g].T
    return out
```

```python
import numpy as np
import concourse.bass as bass
import concourse.mybir as mybir
from concourse.bass_utils import run_bass_kernel_spmd

F32 = mybir.dt.float32
BF16 = mybir.dt.bfloat16
AF = mybir.ActivationFunctionType
ALU = mybir.AluOpType

D = 4096
SEQ = 4096
BATCH = 4
N_META = 16
HD = 64
NQH = 32
NKV = 4
QD = 2048
KVD = 256
CD = 2048
CW = 31
FFN = 11008
IN_DIM = QD + 2 * KVD + 2 * CD + 2 * D
EPS = 1e-6
T = 512
NG = 4
HALO = 128
TS = HALO + T + N_META
KC = D // 128
NCC = CD // 128
NFT = FFN // 128
FG_SIZES = [10, 10, 11, 11, 11, 11, 11, 11]
NCORES = 8

SAME_ENGINE_SYNC = True

C_BIN = 0
C_BCO = C_BIN + 116
C_G1 = C_BCO + 32
C_G2 = C_G1 + 32
C_GF = C_G2 + 32
C_CW = C_GF + 32
C_CB = C_CW + NCC * CW
C_LG = C_CB + NCC
C_LB = C_LG + NCC
C_SK = C_LB + NCC
C_PERM = C_SK + 16
C_ID = C_PERM + 128
NCST = C_ID + 128


def in_units():
    units = []
    for jt in range(16):
        pair, j = jt // 8, jt % 8
        hlo = (2 * pair) * 8 + j
        hhi = (2 * pair + 1) * 8 + j
        cols = np.concatenate([hlo * 64 + np.arange(64), hhi * 64 + np.arange(64)])
        units.append(("q%d" % jt, cols))
    for kp in range(2):
        units.append(("k%d" % kp, QD + kp * 128 + np.arange(128)))
    for u in range(2):
        units.append(("v%d" % u, QD + KVD + u * 128 + np.arange(128)))
    c0 = QD + 2 * KVD
    for cc in range(NCC):
        units.append(("ca%d" % cc, c0 + cc * 128 + np.arange(128)))
        units.append(("cg%d" % cc, c0 + CD + cc * 128 + np.arange(128)))
    g0 = c0 + 2 * CD
    for j in range(KC):
        units.append(("ga%d" % j, g0 + j * 128 + np.arange(128)))
        units.append(("gb%d" % j, g0 + D + j * 128 + np.arange(128)))
    return units


def ot_rows():
    rows = []
    for hk in range(NKV):
        for c in range(4):
            hlo = 8 * hk + c
            hhi = 8 * hk + 4 + c
            rows.append(hlo * 64 + np.arange(64))
            rows.append(hhi * 64 + np.arange(64))
    return np.concatenate(rows)


def unit_schedule():
    sched = []
    iu = in_units()
    for name, _ in iu[:16 + 2 + 2 + 2 * NCC]:
        sched.append((name, 4096))
    for j in range(KC):
        sched.append(("ga%d" % j, 4096))
        sched.append(("gb%d" % j, 4096))
        sched.append(("ao%d" % j, 2048))
        sched.append(("co%d" % j, 2048))
    for j in range(KC):
        sched.append(("wo%d" % j, 4096))
    f0 = 0
    for fg, n in enumerate(FG_SIZES):
        for ft in range(n):
            sched.append(("fgate%d" % (f0 + ft), 4096))
            sched.append(("fup%d" % (f0 + ft), 4096))
        for j in range(KC):
            sched.append(("wd%d_%d" % (fg, j), n * 128))
        f0 += n
    return sched


def _tile_units(w, kchunks):
    K, N = w.shape
    return np.ascontiguousarray(
        w.reshape(kchunks, 128, N // 128, 128).transpose(2, 1, 0, 3)).reshape(N // 128, 128, kchunks * 128)


def pack_weights(w_in, w_ao, w_co, w_out, w_gu, w_down):
    sched = unit_schedule()
    tot = sum(n for _, n in sched)
    wts = np.empty((128, tot), np.float32)
    iu = in_units()
    allcols = np.concatenate([c for _, c in iu])
    win_p = _tile_units(np.ascontiguousarray(w_in[:, allcols]), KC)
    in_idx = {name: i for i, (name, _) in enumerate(iu)}
    ao_u = _tile_units(np.ascontiguousarray(w_ao[ot_rows(), :]), 16)
    co_u = _tile_units(w_co, 16)
    wo_u = _tile_units(w_out, KC)
    gu_u = _tile_units(w_gu, KC)
    off = 0
    f0s = np.cumsum([0] + FG_SIZES)
    for name, n in sched:
        if name in in_idx:
            blk = win_p[in_idx[name]]
        elif name.startswith("ao"):
            blk = ao_u[int(name[2:])]
        elif name.startswith("co"):
            blk = co_u[int(name[2:])]
        elif name.startswith("wo"):
            blk = wo_u[int(name[2:])]
        elif name.startswith("fgate"):
            blk = gu_u[int(name[5:])]
        elif name.startswith("fup"):
            blk = gu_u[NFT + int(name[3:])]
        elif name.startswith("wd"):
            fg, j = name[2:].split("_")
            fg, j = int(fg), int(j)
            f0, nf = f0s[fg], FG_SIZES[fg]
            sub = w_down[f0 * 128:(f0 + nf) * 128, j * 128:(j + 1) * 128]
            blk = sub.reshape(nf, 128, 128).transpose(1, 0, 2).reshape(128, nf * 128)
        else:
            raise KeyError(name)
        wts[:, off:off + n] = blk
        off += n
    assert off == tot
    return wts


def pack_consts(b_in, b_co, g1, g2, gf, conv_w, conv_b, ln_g, ln_b, sinks, b_v_dummy=None):
    cst = np.zeros((128, NCST), np.float32)
    iu = in_units()
    for i, (name, cols) in enumerate(iu):
        cst[:, C_BIN + i] = b_in[cols]
    cst[:, C_BCO:C_BCO + 32] = b_co.reshape(32, 128).T
    cst[:, C_G1:C_G1 + 32] = g1.reshape(32, 128).T
    cst[:, C_G2:C_G2 + 32] = g2.reshape(32, 128).T
    cst[:, C_GF:C_GF + 32] = gf.reshape(32, 128).T
    cw = conv_w.reshape(CW, CD)
    cst[:, C_CW:C_CW + NCC * CW] = cw.reshape(CW, NCC, 128).transpose(2, 1, 0).reshape(128, NCC * CW)
    cst[:, C_CB:C_CB + NCC] = conv_b.reshape(NCC, 128).T
    cst[:, C_LG:C_LG + NCC] = ln_g.reshape(NCC, 128).T
    cst[:, C_LB:C_LB + NCC] = ln_b.reshape(NCC, 128).T
    for hk in range(NKV):
        for c in range(4):
            cst[0:64, C_SK + hk * 4 + c] = sinks[8 * hk + c]
            cst[64:128, C_SK + hk * 4 + c] = sinks[8 * hk + 4 + c]
    perm = np.zeros((128, 128), np.float32)
    for m in range(128):
        partner = m + 32 if (m % 64) < 32 else m - 32
        perm[partner, m] = 1.0
    cst[:, C_PERM:C_PERM + 128] = perm
    cst[:, C_ID:C_ID + 128] = np.eye(128, dtype=np.float32)
    return cst


def rope_tables(pos):
    inv_freq = (np.float32(10000.0) ** (-np.arange(0, HD, 2, dtype=np.float32) / np.float32(HD))).astype(np.float32)
    ang = (pos[:, None].astype(np.float32) * inv_freq[None, :]).astype(np.float32)
    cos = np.cos(ang).astype(np.float32)
    sin = np.sin(ang).astype(np.float32)
    m = np.arange(128)
    f = m % 32
    sign = np.where((m % 64) < 32, -1.0, 1.0).astype(np.float32)
    cosT = np.ascontiguousarray(cos[:, f].T)
    sinT = np.ascontiguousarray(sin[:, f].T * sign[:, None])
    return cosT, sinT


def core_inputs(core, x, meta):
    b, half = core // 2, core % 2
    xT = np.empty((NG, 128, KC, TS), np.float32)
    tabs = np.empty((NG, 128, 2, TS), np.float32)
    masks = np.empty((NG, 128, 3 * 512), np.float32)
    cmask = np.ones((NG, 128, 128), np.float32)
    jj = np.arange(128)[:, None]
    ii = np.arange(128)[None, :]
    mprev = (jj > ii).astype(np.float32)
    mcur = (jj <= ii).astype(np.float32)
    for g in range(NG):
        s0 = half * 2048 + g * T
        slab = np.zeros((TS, D), np.float32)
        pos = np.zeros((TS,), np.float32)
        if s0 >= HALO:
            slab[0:HALO] = x[b, s0 - HALO:s0]
            pos[0:HALO] = N_META + np.arange(s0 - HALO, s0)
            m0 = mprev
        else:
            slab[HALO - N_META:HALO] = meta
            pos[HALO - N_META:HALO] = np.arange(N_META)
            m0 = np.zeros_like(mprev)
            cmask[g, :, 0:HALO - N_META] = 0.0
        slab[HALO:HALO + T] = x[b, s0:s0 + T]
        pos[HALO:HALO + T] = N_META + np.arange(s0, s0 + T)
        slab[HALO + T:] = meta
        pos[HALO + T:] = np.arange(N_META)
        xT[g] = slab.T.reshape(KC, 128, TS).transpose(1, 0, 2)
        c, s = rope_tables(pos)
        tabs[g, :, 0, :] = c
        tabs[g, :, 1, :] = s
        masks[g, :, 0:512] = np.tile(m0, (1, 4))
        masks[g, :, 512:1024] = np.tile(mprev, (1, 4))
        masks[g, :, 1024:1536] = np.tile(mcur, (1, 4))
    return {"xT": xT, "tabs": tabs, "masks": masks, "cmask": cmask}


class Tok:
    __slots__ = ("sem", "val", "eng", "key")

    def __init__(self, sem, val, eng, key):
        self.sem, self.val, self.eng, self.key = sem, val, eng, key


class Eng:
    def __init__(self, name, sem, is_pe=False):
        self.name, self.sem, self.is_pe = name, sem, is_pe
        self.n = 0
        self.known = {}
        self.prog = []

    def record(self, fn, deps, inc=True):
        waits = []
        for d in deps:
            if d is None:
                continue
            if self.known.get(d.key, 0) >= d.val:
                continue
            self.known[d.key] = d.val
            waits.append((d.sem, d.val))
        tok = None
        if inc:
            self.n += 1
            tok = Tok(self.sem, self.n, self, "E" + self.name)
        self.prog.append((waits, fn, inc))
        return tok

    def replay(self, h):
        for waits, fn, inc in self.prog:
            for s, v in waits:
                h.wait_ge(s, v)
            if fn is None:
                continue
            ins = fn(h)
            if inc:
                ins.then_inc(self.sem, 1)


class Res:
    __slots__ = ("w", "r")

    def __init__(self):
        self.w = None
        self.r = {}


def res_tokens(rs):
    out = []
    for r in rs:
        if r.w is not None:
            out.append(r.w)
        out.extend(r.r.values())
    return out


def op(eng, fn, reads=(), writes=(), extra=(), inc=True, tok_override=None):
    deps = []
    for r in reads:
        if r.w is not None:
            deps.append(("raw", r.w))
    for w in writes:
        if w.w is not None:
            deps.append(("waw", w.w))
        for t in w.r.values():
            deps.append(("war", t))
    for t in extra:
        deps.append(("raw", t))
    fl = []
    for kind, t in deps:
        if t.eng is eng:
            if eng.is_pe:
                continue
            if not (kind == "raw" and SAME_ENGINE_SYNC):
                continue
        fl.append(t)
    tok = eng.record(fn, fl, inc)
    if tok_override is not None:
        tok = tok_override
    if tok is not None:
        for r in reads:
            r.r[tok.key] = tok
        for w in writes:
            w.w = tok
            w.r = {}
    return tok


class DmaSlot:
    def __init__(self, sem, name):
        self.sem, self.name, self.cnt = sem, name, 0
        self.res = Res()

    def next_tok(self):
        self.cnt += 1
        return Tok(self.sem, 16 * self.cnt, None, "D" + self.name)


def dma(queue, slot, out_ap, in_ap, reads=(), writes=(), extra=()):
    tok = slot.next_tok()
    sem = slot.sem

    def fn(h):
        return h.dma_start(out=out_ap, in_=in_ap).then_inc(sem, 16)

    op(queue, fn, reads=reads, writes=writes, extra=extra, inc=False, tok_override=tok)
    return tok


def build_nc(ng=NG, debug=False):
    nc = bass.Bass("TRN2", target_bir_lowering=False)
    sched = unit_schedule()
    tot_cols = sum(n for _, n in sched)
    xT_d = nc.dram_tensor("xT", [ng, 128, KC, TS], F32, kind="ExternalInput").ap()
    tabs_d = nc.dram_tensor("tabs", [ng, 128, 2, TS], F32, kind="ExternalInput").ap()
    masks_d = nc.dram_tensor("masks", [ng, 128, 1536], F32, kind="ExternalInput").ap()
    cmask_d = nc.dram_tensor("cmask", [ng, 128, 128], F32, kind="ExternalInput").ap()
    cst_d = nc.dram_tensor("cst", [128, NCST], F32, kind="ExternalInput").ap()
    bv_d = nc.dram_tensor("bv", [128, 256], F32, kind="ExternalInput").ap()
    wts_d = nc.dram_tensor("wts", [128, tot_cols], F32, kind="ExternalInput").ap()
    out_d = nc.dram_tensor("outT", [ng, D, T], F32, kind="ExternalOutput").ap()
    dbg = {}
    if debug:
        for nm, shp, dt_ in [("d_u", [128, KC * TS], BF16), ("d_q", [128, 16 * T], BF16), ("d_k", [128, 2 * TS], BF16),
                             ("d_ot", [128, 16 * T], BF16), ("d_cs", [128, 16 * T], BF16), ("d_mg", [128, 32 * T], BF16),
                             ("d_h1", [128, 32 * T], F32), ("d_y", [128, 16 * T], F32)]:
            dbg[nm] = nc.dram_tensor(nm, shp, dt_, kind="ExternalOutput").ap()

    from contextlib import ExitStack
    with ExitStack() as es:
        def sb(name, shape, dt):
            return es.enter_context(nc.sbuf_tensor(name, shape, dt))

        def sem(name):
            return es.enter_context(nc.semaphore(name))

        R1 = sb("R1", [128, 16384], F32)
        R2 = sb("R2", [128, 8192], F32)
        U = sb("U", [128, KC, TS], BF16)
        WS = [sb("W%d" % i, [128, 4096], BF16) for i in range(4)]
        XR = [sb("XR%d" % i, [128, TS], F32) for i in range(3)]
        SQ = [sb("SQ%d" % i, [128, TS], BF16) for i in range(2)]
        RSTD = sb("RSTD", [128, TS], F32)
        ST = [sb("ST%d" % i, [128, T], F32) for i in range(3)]
        MSK = sb("MSK", [128, 1536], BF16)
        CMK = sb("CMK", [128, 128], F32)
        CST = sb("CST", [128, NCST], F32)
        BV = sb("BV", [128, 256], F32)
        ONEB = sb("ONEB", [128, 128], BF16)
        ONEF = sb("ONEF", [128, 128], F32)
        ONELH = sb("ONELH", [128, 2, 128], BF16)
        ESK = sb("ESK", [128, 16], F32)
        IDB = sb("IDB", [128, 128], BF16)
        FT = [sb("FT%d" % i, [128, T], F32) for i in range(4)]
        banks = [es.enter_context(nc.psum_tensor("PS%d" % i, [128, 512], F32)) for i in range(8)]

        H1 = R1[:, :].rearrange("p (a b) -> p a b", b=T)
        QT = R1[:, 0:4096].bitcast(BF16).rearrange("p (a b) -> p a b", b=T)
        CSB = QT
        OT = R1[:, 4096:8192].bitcast(BF16).rearrange("p (a b) -> p a b", b=T)
        KT = R1[:, 8192:8192 + TS].bitcast(BF16).rearrange("p (a b) -> p a b", b=TS)
        VA = R1[:, 8848:8848 + 3072].bitcast(BF16).rearrange("p (t h v c) -> p t h v c", t=6, h=4, v=2)
        VA_flat = R1[:, 8848:8848 + 3072].bitcast(BF16)
        CB = [R1[:, 11920 + i * 320:11920 + (i + 1) * 320].bitcast(BF16) for i in range(2)]
        NDG = 6
        DG = [R1[:, 15792 + i * 64:15792 + (i + 1) * 64].bitcast(BF16) for i in range(NDG)]
        SG = [R1[:, 13200 + i * 640:13200 + (i + 1) * 640] for i in range(2)]
        QF = [R1[:, 14480 + i * TS:14480 + (i + 1) * TS] for i in range(2)]
        TAB = R2[:, 0:2 * TS].rearrange("p (a b) -> p a b", b=TS)
        RT = [R2[:, 1312 + i * TS:1312 + (i + 1) * TS] for i in range(2)]
        ET = [R2[:, 4096 + i * 256:4096 + (i + 1) * 256].bitcast(BF16) for i in range(4)]
        PT = [R2[:, 5120 + i * 256:5120 + (i + 1) * 256].bitcast(BF16) for i in range(8)]
        PM = [R2[:, 7168 + i * 256:7168 + (i + 1) * 256].bitcast(BF16) for i in range(4)]
        Y = R2[:, :].rearrange("p (a b) -> p a b", b=T)
        MG = R2[:, :].bitcast(BF16).rearrange("p (a b) -> p a b", b=T)
        ACT = [R2[:, i * 2816:(i + 1) * 2816].bitcast(BF16).rearrange("p (a b) -> p a b", b=T) for i in range(2)]

        PE = Eng("pe", sem("s_pe"), is_pe=True)
        AC = Eng("act", sem("s_act"))
        DV = Eng("dve", sem("s_dve"))
        PO = Eng("pool", sem("s_pool"))
        SP = Eng("sp", sem("s_sp"))
        wslots = [DmaSlot(sem("s_w%d" % i), "w%d" % i) for i in range(4)]
        xslots = [DmaSlot(sem("s_x%d" % i), "x%d" % i) for i in range(3)]
        cslot = DmaSlot(sem("s_c"), "c")
        tslot = DmaSlot(sem("s_t"), "t")
        mslot = DmaSlot(sem("s_m"), "m")
        oslots = [DmaSlot(sem("s_o%d" % i), "o%d" % i) for i in range(4)]
        dslots = [DmaSlot(sem("s_d%d" % i), "d%d" % i) for i in range(8)] if debug else []
        dstate = {"i": 0}

        def next_dslot():
            d = dslots[dstate["i"]]
            dstate["i"] += 1
            return d

        r_u = Res(); r_qcs = Res(); r_ot = Res(); r_kt = Res(); r_va = Res()
        r_dg = [Res() for _ in range(6)]
        r_cb = [Res(), Res()]; r_sg = [Res(), Res()]; r_qf = [Res(), Res()]
        r_rt = [Res(), Res()]; r_et = [Res() for _ in range(4)]; r_pt = [Res() for _ in range(8)]
        r_pm = [Res() for _ in range(4)]
        r_y = [Res() for _ in range(NCC)]
        r_mg = Res()
        r_act = [Res(), Res()]
        r_h1 = [Res() for _ in range(KC)]
        r_sq = [Res(), Res()]
        r_rstd = Res()
        r_st = [Res() for _ in range(3)]
        r_ft = [Res() for _ in range(4)]
        r_bank = [Res() for _ in range(8)]
        r_cst = Res(); r_const = Res()
        r_tab = Res(); r_msk = Res()
        state = {"bank": 0, "ft": 0, "x": 0, "sq": 0, "w": 0, "o": 0}

        pinned = set()

        def next_bank():
            b = state["bank"]
            while b in pinned:
                b = (b + 1) % 6
            state["bank"] = (b + 1) % 6
            return banks[b], r_bank[b]

        def bank_index(bk_res):
            return r_bank.index(bk_res)

        def next_ft():
            i = state["ft"]
            state["ft"] = (i + 1) % 4
            return FT[i], r_ft[i]

        woffs = np.cumsum([0] + [n for _, n in sched])
        wstate = {"issued": 0, "used": 0, "goff": 0}
        nunits = len(sched)
        total_units = nunits * ng

        def w_issue_upto(k):
            while wstate["issued"] < min(k, total_units):
                i = wstate["issued"]
                u = i % nunits
                slot = wslots[i % 4]
                n = sched[u][1]
                dma(PO, slot, WS[i % 4][:, 0:n], wts_d[:, int(woffs[u]):int(woffs[u]) + n],
                    writes=[slot.res])
                wstate["issued"] += 1

        def w_next(name):
            i = wstate["used"]
            u = i % nunits
            assert sched[u][0] == name, (sched[u][0], name)
            w_issue_upto(i + 3)
            wstate["used"] += 1
            return WS[i % 4], wslots[i % 4].res, i

        def w_done():
            w_issue_upto(wstate["used"] + 3)

        def mm_chain(out_ap, pairs, reads, bres, extra=()):
            n = len(pairs)

            def fn(h):
                ins = None
                for i, (l, r) in enumerate(pairs):
                    ins = h.matmul(out_ap, l, r, start=(i == 0), stop=(i == n - 1))
                return ins
            return op(PE, fn, reads=reads, writes=[bres], extra=extra)

        def act(out_ap, in_ap, func, reads, writes, bias=None, scale=None, extra=()):
            kw = {}
            if bias is not None:
                kw["bias"] = bias
            if scale is not None:
                kw["scale"] = scale
            return op(AC, lambda h: h.activation(out_ap, in_ap, func, **kw), reads=reads, writes=writes, extra=extra)

        def tt(out_ap, a, b, alu, reads, writes, extra=()):
            return op(DV, lambda h: h.tensor_tensor(out_ap, a, b, alu), reads=reads, writes=writes, extra=extra)

        def stt(out_ap, in0, scalar, in1, op0, op1, reads, writes, extra=()):
            return op(DV, lambda h: h.scalar_tensor_tensor(out_ap, in0, scalar, in1, op0, op1),
                      reads=reads, writes=writes, extra=extra)

        def ts(out_ap, in0, s1, s2, op0, op1, reads, writes, extra=()):
            if s2 is None:
                return op(DV, lambda h: h.tensor_scalar(out_ap, in0, s1, None, op0), reads=reads, writes=writes, extra=extra)
            return op(DV, lambda h: h.tensor_scalar(out_ap, in0, s1, s2, op0, op1), reads=reads, writes=writes, extra=extra)

        def cc_(c0, n=1):
            return CST[:, c0:c0 + n]

        dma(SP, cslot, CST[:, :], cst_d[:, :], writes=[r_cst])
        dma(SP, cslot, BV[:, :], bv_d[:, :], writes=[r_cst])
        op(DV, lambda h: h.memset(ONEB[:, :], 1.0), writes=[r_const])
        op(DV, lambda h: h.memset(ONEF[:, :], 1.0), writes=[r_const])
        op(DV, lambda h: h.memset(ONELH[:, :, :], 0.0), writes=[r_const])
        op(DV, lambda h: h.memset(ONELH[:, 0, 0:64], 1.0), writes=[r_const])
        op(DV, lambda h: h.memset(ONELH[:, 1, 64:128], 1.0), writes=[r_const])
        act(ESK[:, :], cc_(C_SK, 16), AF.Exp, reads=[r_cst], writes=[r_const])
        op(DV, lambda h: h.tensor_copy(IDB[:, :], CST[:, C_ID:C_ID + 128]), reads=[r_cst], writes=[r_const])


        EPSB = sb("EPSB", [128, 1], F32)
        CARRY = sb("CARRY", [128, NCC, 32], BF16)
        r_carry = [Res() for _ in range(NCC)]
        if debug:
            print("sbuf bytes remaining", nc.sbuf_bytes_remaining)
        op(DV, lambda h: h.memset(EPSB[:, :], EPS), writes=[r_const])

        def mm1(out_ap, l, r, st, sp_):
            return lambda h: h.matmul(out_ap, l, r, start=st, stop=sp_)

        def recip(ap, reads, writes):
            return op(DV, lambda h: h.reciprocal(ap, ap), reads=reads, writes=writes)

        def xload(src_ap, n):
            xi = state["x"]
            state["x"] = (xi + 1) % 3
            dma(SP, xslots[xi], XR[xi][:, 0:n], src_ap, writes=[xslots[xi].res])
            return XR[xi], xslots[xi].res

        def next_sq():
            si = state["sq"]
            state["sq"] ^= 1
            return SQ[si], r_sq[si]

        def wtile(W, i):
            return W[:, i * 128:(i + 1) * 128]

        fn_q = []
        fn_done = set()

        def emit_fn(n):
            for _ in range(min(n, len(fn_q))):
                fn_q.pop(0)()

        def stats_chunk(gg, kc):
            xr, xres = xload(xT_d[gg, :, kc, 128:640], 512)
            sq, sqr = next_sq()
            act(sq[:, 0:512], xr[:, 0:512], AF.Square, reads=[xres], writes=[sqr])
            op(PE, mm1(banks[6][:, :], ONEB[:, :], sq[:, 0:512], kc == 0, kc == KC - 1),
               reads=[sqr, r_const], writes=[r_bank[6]])

        def stats_finish_own():
            act(RSTD[:, 128:640], banks[6][:, :], AF.Ln, reads=[r_bank[6], r_const], writes=[r_rstd],
                bias=EPSB[:, 0:1], scale=1.0 / D)
            act(RSTD[:, 128:640], RSTD[:, 128:640], AF.Exp, reads=[r_rstd], writes=[r_rstd], scale=-0.5)

        def stats_finish():
            act(RSTD[:, 0:512], banks[6][:, :], AF.Ln, reads=[r_bank[6], r_const], writes=[r_rstd],
                bias=EPSB[:, 0:1], scale=1.0 / D)
            act(RSTD[:, 512:656], banks[7][:, 0:144], AF.Ln, reads=[r_bank[7], r_const], writes=[r_rstd],
                bias=EPSB[:, 0:1], scale=1.0 / D)
            act(RSTD[:, :], RSTD[:, :], AF.Exp, reads=[r_rstd], writes=[r_rstd], scale=-0.5)

        def u_chunk(gg, kc):
            op(DV, lambda h, kc=kc: h.tensor_copy(U[:, kc, 0:128], U[:, kc, 512:640]), reads=[r_u], writes=[r_u])
            xr, xres = xload(xT_d[gg, :, kc, 128:640], 512)
            stt(U[:, kc, 128:640], xr[:, 0:512], cc_(C_G1 + kc), RSTD[:, 128:640], ALU.mult, ALU.mult,
                reads=[xres, r_rstd, r_cst], writes=[r_u])

        for g in range(ng):
            emit_fn(6)
            pro_tokens = []
            if g == 0:
                pslots = [DmaSlot(sem("s_p%d" % q), "p%d" % q) for q in range(4)]
                xviews = []
                for q in range(4):
                    base = R1[:, q * 8 * TS:(q + 1) * 8 * TS] if q < 3 else R2[:, 0:8 * TS]
                    v = base.rearrange("p (a b) -> p a b", b=TS)
                    dma(SP, pslots[q], v, xT_d[0, :, 8 * q:8 * q + 8, :], writes=[pslots[q].res])
                    xviews.extend((v[:, i, :], pslots[q].res) for i in range(8))
                for kc in range(KC):
                    xv, xvr = xviews[kc]
                    sq, sqr = next_sq()
                    act(sq[:, :], xv, AF.Square, reads=[xvr], writes=[sqr])
                    op(PE, mm1(banks[6][:, :], ONEB[:, :], sq[:, 0:512], kc == 0, kc == KC - 1),
                       reads=[sqr, r_const], writes=[r_bank[6]])
                    op(PE, mm1(banks[7][:, 0:144], ONEB[:, :], sq[:, 512:656], kc == 0, kc == KC - 1),
                       reads=[sqr, r_const], writes=[r_bank[7]])
                stats_finish()
                for kc in range(KC):
                    xv, xvr = xviews[kc]
                    stt(U[:, kc, :], xv, cc_(C_G1 + kc), RSTD[:, :], ALU.mult, ALU.mult,
                        reads=[xvr, r_rstd, r_cst], writes=[r_u])
                pro_tokens = res_tokens([p_.res for p_ in pslots])
            fenceR1 = res_tokens(r_h1) + pro_tokens
            fenceQF = res_tokens(r_h1[28:31]) + pro_tokens
            fenceR2 = res_tokens(r_act) + res_tokens([r_mg]) + pro_tokens
            dma(SP, tslot, TAB[:, :, :], tabs_d[g], writes=[r_tab], extra=fenceR2)
            dma(SP, tslot, CMK[:, :], cmask_d[g], writes=[r_tab])
            dma(PO, mslot, MSK[:, :], masks_d[g], writes=[r_msk])
            if debug and g == 0:
                dma(SP, next_dslot(), dbg["d_u"], U[:, :, :].rearrange("p a b -> p (a b)"), reads=[r_u])

            def rope_finish(dst_ap, dres, qf, qfr, segs, tok_lo, tok_hi, dfence=None):
                outs = []
                for (lo, hi) in segs:
                    b2, b2r = next_bank()
                    op(PE, mm1(b2[:, 0:hi - lo], CST[:, C_PERM:C_PERM + 128], qf[:, lo:hi], True, True),
                       reads=[qfr, r_cst], writes=[b2r])
                    outs.append((b2, b2r, lo, hi))
                n = tok_hi - tok_lo
                tt(RT[0][:, 0:n], qf[:, 0:n], TAB[:, 0, tok_lo:tok_hi], ALU.mult,
                   reads=[qfr, r_tab], writes=[r_rt[0]], extra=fenceR2)
                for (b2, b2r, lo, hi) in outs:
                    tt(RT[1][:, lo:hi], b2[:, 0:hi - lo], TAB[:, 1, tok_lo + lo:tok_lo + hi], ALU.mult,
                       reads=[b2r, r_tab], writes=[r_rt[1]], extra=fenceR2)
                tt(dst_ap, RT[0][:, 0:n], RT[1][:, 0:n], ALU.add, reads=[r_rt[0], r_rt[1]], writes=[dres],
                   extra=fenceR1 if dfence is None else dfence)

            def rope_q(jt, qi):
                assert g == 0 or not fn_q or (jt // 2) in fn_done, (jt, sorted(fn_done))
                rope_finish(QT[:, jt, :], r_qcs, QF[qi], r_qf[qi], [(0, 512)], 128, 640, res_tokens([r_h1[jt // 2]]) + pro_tokens)

            pend = None
            for jt in range(16):
                W, wres, _ = w_next("q%d" % jt)
                bk, br = next_bank()
                mm_chain(bk[:, :], [(wtile(W, kc), U[:, kc, 128:640]) for kc in range(KC)],
                         reads=[wres, r_u], bres=br)
                w_done()
                qi = jt % 2
                act(QF[qi][:, 0:512], bk[:, :], AF.Identity, reads=[br, r_cst], writes=[r_qf[qi]],
                    bias=cc_(C_BIN + jt), extra=fenceQF)
                if pend is not None:
                    rope_q(*pend)
                pend = (jt, qi)
                emit_fn(2)
            rope_q(*pend)
            emit_fn(len(fn_q))
            fenceR1 = res_tokens(r_h1) + pro_tokens
            for kp in range(2):
                W, wres, _ = w_next("k%d" % kp)
                bA, brA = next_bank()
                bB, brB = next_bank()
                mm_chain(bA[:, :], [(wtile(W, kc), U[:, kc, 0:512]) for kc in range(KC)], reads=[wres, r_u], bres=brA)
                mm_chain(bB[:, 0:144], [(wtile(W, kc), U[:, kc, 512:656]) for kc in range(KC)], reads=[wres, r_u], bres=brB)
                w_done()
                qi = kp % 2
                act(QF[qi][:, 0:512], bA[:, :], AF.Identity, reads=[brA, r_cst], writes=[r_qf[qi]],
                    bias=cc_(C_BIN + 16 + kp), extra=fenceR1)
                act(QF[qi][:, 512:656], bB[:, 0:144], AF.Identity, reads=[brB, r_cst], writes=[r_qf[qi]],
                    bias=cc_(C_BIN + 16 + kp), extra=fenceR1)
                rope_finish(KT[:, kp, :], r_kt, QF[qi], r_qf[qi], [(0, 512), (512, 656)], 0, 656)

            op(DV, lambda h: h.memset(VA_flat[:, :], 0.0), writes=[r_va], extra=fenceR1)
            for u in range(2):
                W, wres, _ = w_next("v%d" % u)
                bA, brA = next_bank()
                bB, brB = next_bank()
                outs = []
                for tb in range(6):
                    M = 128 if tb < 5 else 16
                    t0 = tb * 128
                    if tb < 4:
                        o_ap, o_r = bA[0:M, tb * 128:(tb + 1) * 128], brA
                    else:
                        o_ap, o_r = bB[0:M, (tb - 4) * 128:(tb - 3) * 128], brB
                    mm_chain(o_ap, [(U[:, kc, t0:t0 + M], wtile(W, kc)) for kc in range(KC)],
                             reads=[wres, r_u], bres=o_r)
                    outs.append((o_ap, o_r, M))
                w_done()
                for tb, (o_ap, o_r, M) in enumerate(outs):
                    for hh in range(2):
                        hk = 2 * u + hh
                        tt(VA[0:M, tb, hk, 0, 0:64], o_ap[:, hh * 64:(hh + 1) * 64], BV[0:M, hk * 64:(hk + 1) * 64],
                           ALU.add, reads=[o_r, r_cst], writes=[r_va])
                        tt(VA[0:M, tb, hk, 1, 64:128], o_ap[:, hh * 64:(hh + 1) * 64], BV[0:M, hk * 64:(hk + 1) * 64],
                           ALU.add, reads=[o_r, r_cst], writes=[r_va])

            def att_scores(blk, hk):
                P0 = (hk % 2) * 64
                kp = hk // 2
                tbase = kp * 8
                keyspec = [(KT[P0:P0 + 64, kp, blk * 128:(blk + 1) * 128], 128, blk, 0 if blk == 0 else 1),
                           (KT[P0:P0 + 64, kp, (blk + 1) * 128:(blk + 2) * 128], 128, blk + 1, 2),
                           (KT[P0:P0 + 64, kp, 640:656], 16, 5, None)]
                ptiles = []
                par = (blk * 4 + hk) % 2
                for ki, (kap, M, tbi, mi) in enumerate(keyspec):
                    for half in range(2):
                        bk, br = next_bank()
                        rhs = QT[P0:P0 + 64, tbase + half * 4:tbase + half * 4 + 4, blk * 128:(blk + 1) * 128]
                        o3 = bk[0:M, :].rearrange("p (a b) -> p a b", a=4)
                        mm_chain(o3, [(kap, rhs)], reads=[r_kt, r_qcs], bres=br)
                        if mi is not None:
                            ei = ki * 2 + half
                            act(ET[ei][:, :], bk[:, :], AF.Exp, reads=[br], writes=[r_et[ei]], scale=0.125,
                                extra=fenceR2)
                            pi = ei + 4 * par
                            tt(PT[pi][:, :], ET[ei][:, :], MSK[:, mi * 512:(mi + 1) * 512], ALU.mult,
                               reads=[r_et[ei], r_msk], writes=[r_pt[pi]], extra=fenceR2)
                            ptiles.append((PT[pi][:, :], r_pt[pi], 128, tbi, half))
                        else:
                            mi_ = half + 2 * par
                            act(PM[mi_][0:16, :], bk[0:16, :], AF.Exp, reads=[br], writes=[r_pm[mi_]], scale=0.125,
                                extra=fenceR2)
                            ptiles.append((PM[mi_][0:16, :], r_pm[mi_], 16, 5, half))
                return ptiles

            def att_pv(blk, hk, ptiles):
                bO, brO = next_bank()
                bD, brD = next_bank()
                for i, (pap, pres, M, tbi, half) in enumerate(ptiles):
                    op(PE, mm1(bO[:, :], VA[0:M, tbi, hk, half, :], pap, i == 0, i == 5),
                       reads=[pres, r_va], writes=[brO])
                for i, (pap, pres, M, tbi, half) in enumerate(ptiles):
                    op(PE, mm1(bD[:, :], ONELH[0:M, half, :], pap, i == 0, i == 5),
                       reads=[pres, r_const], writes=[brD])
                f, fr = next_ft()
                for c in range(4):
                    act(f[:, c * 128:(c + 1) * 128], bD[:, c * 128:(c + 1) * 128], AF.Ln, reads=[brD, r_const], writes=[fr],
                        bias=ESK[:, hk * 4 + c:hk * 4 + c + 1])
                act(f[:, :], f[:, :], AF.Exp, reads=[fr], writes=[fr], scale=-1.0)
                tt(OT[:, hk * 4:hk * 4 + 4, blk * 128:(blk + 1) * 128], bO[:, :].rearrange("p (c q) -> p c q", c=4),
                   f[:, :].rearrange("p (c q) -> p c q", c=4), ALU.mult, reads=[brO, fr], writes=[r_ot], extra=fenceR1)

            its = [(blk, hk) for blk in range(4) for hk in range(4)]
            att_state = {"i": 0, "prev": None}

            def att_step():
                i = att_state["i"]
                if i < len(its):
                    blk, hk = its[i]
                    pt_ = att_scores(blk, hk)
                    if att_state["prev"] is not None:
                        att_pv(*att_state["prev"])
                    att_state["prev"] = (blk, hk, pt_)
                    att_state["i"] = i + 1
                elif att_state["prev"] is not None:
                    att_pv(*att_state["prev"])
                    att_state["prev"] = None

            fence_y_lo = res_tokens(r_rt + [r_tab])

            def fence_y_for(cc):
                if cc < 8:
                    return fence_y_lo
                assert att_state["i"] == len(its) and att_state["prev"] is None
                return fence_y_lo + res_tokens(r_et + r_pt + r_pm)

            def conv_pe(cc, ci):
                bk, br = next_bank()
                for k in range(CW):
                    di = dgs["i"]
                    dgs["i"] = (di + 1) % NDG
                    ts(DG[di][:, :], IDB[:, :], cc_(C_CW + cc * CW + k), None, ALU.mult, None,
                       reads=[r_const, r_cst], writes=[r_dg[di]], extra=fenceR1)
                    op(PE, mm1(bk[:, :], DG[di][:, :], CB[ci][:, 98 + k:610 + k], k == 0, k == CW - 1),
                       reads=[r_dg[di], r_cb[ci]], writes=[br])
                act(Y[:, cc, :], bk[:, :], AF.Identity, reads=[br, r_cst], writes=[r_y[cc]], bias=cc_(C_CB + cc),
                    extra=fence_y_for(cc))
                f, fr = next_ft()
                act(f[:, :], Y[:, cc, :], AF.Square, reads=[r_y[cc]], writes=[fr])
                if stq:
                    ln_stat_mm(*stq.pop(0))
                stq.append((cc, f, fr))

            def ln_stat_mm(cc, f, fr):
                op(PE, mm1(banks[6][:, :], ONEF[:, :], Y[:, cc, :], cc == 0, cc == NCC - 1),
                   reads=[r_y[cc], r_const], writes=[r_bank[6]])
                op(PE, mm1(banks[7][:, :], ONEF[:, :], f[:, :], cc == 0, cc == NCC - 1),
                   reads=[fr, r_const], writes=[r_bank[7]])

            stq = []
            dgs = {"i": 0}
            pendc = None
            for cc in range(NCC):
                ci = cc % 2
                ua = C_BIN + 20 + 2 * cc
                ug = ua + 1
                if g == 0:
                    W, wres, _ = w_next("ca%d" % cc)
                    bA0, rA0 = next_bank()
                    bA1, rA1 = next_bank()
                    mm_chain(bA0[:, :], [(wtile(W, kc), U[:, kc, 0:512]) for kc in range(KC)], reads=[wres, r_u], bres=rA0)
                    mm_chain(bA1[:, 0:128], [(wtile(W, kc), U[:, kc, 512:640]) for kc in range(KC)], reads=[wres, r_u], bres=rA1)
                    w_done()
                    pinned.update([bank_index(rA0), bank_index(rA1)])
                    att_step()
                    W, wres, _ = w_next("cg%d" % cc)
                    bG0, rG0 = next_bank()
                    bG1, rG1 = next_bank()
                    mm_chain(bG0[:, :], [(wtile(W, kc), U[:, kc, 0:512]) for kc in range(KC)], reads=[wres, r_u], bres=rG0)
                    mm_chain(bG1[:, 0:128], [(wtile(W, kc), U[:, kc, 512:640]) for kc in range(KC)], reads=[wres, r_u], bres=rG1)
                    w_done()
                    act(SG[ci][:, 0:512], bG0[:, :], AF.Sigmoid, reads=[rG0, r_cst], writes=[r_sg[ci]], bias=cc_(ug), extra=fenceR1)
                    act(SG[ci][:, 512:640], bG1[:, 0:128], AF.Sigmoid, reads=[rG1, r_cst], writes=[r_sg[ci]], bias=cc_(ug), extra=fenceR1)
                    if pendc is not None:
                        conv_pe(*pendc)
                    stt(CB[ci][:, 0:512], bA0[:, :], cc_(ua), SG[ci][:, 0:512], ALU.add, ALU.mult,
                        reads=[rA0, r_sg[ci], r_cst], writes=[r_cb[ci]], extra=fenceR1)
                    stt(CB[ci][:, 512:640], bA1[:, 0:128], cc_(ua), SG[ci][:, 512:640], ALU.add, ALU.mult,
                        reads=[rA1, r_sg[ci], r_cst], writes=[r_cb[ci]], extra=fenceR1)
                    tt(CB[ci][:, 0:128], CB[ci][:, 0:128], CMK[:, :], ALU.mult, reads=[r_cb[ci], r_tab], writes=[r_cb[ci]])
                    pinned.clear()
                else:
                    W, wres, _ = w_next("ca%d" % cc)
                    bA0, rA0 = next_bank()
                    mm_chain(bA0[:, :], [(wtile(W, kc), U[:, kc, 128:640]) for kc in range(KC)], reads=[wres, r_u], bres=rA0)
                    w_done()
                    pinned.add(bank_index(rA0))
                    att_step()
                    W, wres, _ = w_next("cg%d" % cc)
                    bG0, rG0 = next_bank()
                    mm_chain(bG0[:, :], [(wtile(W, kc), U[:, kc, 128:640]) for kc in range(KC)], reads=[wres, r_u], bres=rG0)
                    w_done()
                    act(SG[ci][:, 128:640], bG0[:, :], AF.Sigmoid, reads=[rG0, r_cst], writes=[r_sg[ci]], bias=cc_(ug), extra=fenceR1)
                    if pendc is not None:
                        conv_pe(*pendc)
                    op(DV, lambda h, cc=cc, ci=ci: h.tensor_copy(CB[ci][:, 98:128], CARRY[:, cc, 0:30]),
                       reads=[r_carry[cc]], writes=[r_cb[ci]], extra=fenceR1)
                    stt(CB[ci][:, 128:640], bA0[:, :], cc_(ua), SG[ci][:, 128:640], ALU.add, ALU.mult,
                        reads=[rA0, r_sg[ci], r_cst], writes=[r_cb[ci]], extra=fenceR1)
                    pinned.clear()
                if g + 1 < ng:
                    op(DV, lambda h, cc=cc, ci=ci: h.tensor_copy(CARRY[:, cc, 0:30], CB[ci][:, 610:640]),
                       reads=[r_cb[ci]], writes=[r_carry[cc]])
                pendc = (cc, ci)
                att_step()
            conv_pe(*pendc)
            assert att_state["i"] == len(its) and att_state["prev"] is None
            if debug and g == 0:
                dma(SP, next_dslot(), dbg["d_q"], QT[:, :, :].rearrange("p a b -> p (a b)"), reads=[r_qcs])
                dma(SP, next_dslot(), dbg["d_k"], KT[:, :, :].rearrange("p a b -> p (a b)"), reads=[r_kt])
                dma(SP, next_dslot(), dbg["d_ot"], OT[:, :, :].rearrange("p a b -> p (a b)"), reads=[r_ot])
            while stq:
                ln_stat_mm(*stq.pop(0))
            ts(ST[0][:, :], banks[6][:, :], 1.0 / CD, None, ALU.mult, None, reads=[r_bank[6]], writes=[r_st[0]])
            tt(ST[2][:, :], ST[0][:, :], ST[0][:, :], ALU.mult, reads=[r_st[0]], writes=[r_st[2]])
            stt(ST[2][:, :], banks[7][:, :], 1.0 / CD, ST[2][:, :], ALU.mult, ALU.subtract,
                reads=[r_bank[7], r_st[2]], writes=[r_st[2]])
            act(ST[1][:, :], ST[2][:, :], AF.Ln, reads=[r_st[2], r_const], writes=[r_st[1]], bias=EPSB[:, 0:1], scale=1.0)
            act(ST[1][:, :], ST[1][:, :], AF.Exp, reads=[r_st[1]], writes=[r_st[1]], scale=-0.5)
            if debug and g == 0:
                dma(SP, next_dslot(), dbg["d_y"], Y[:, :, :].rearrange("p a b -> p (a b)"), reads=r_y)
            for cc in range(NCC):
                f, fr = next_ft()
                tt(f[:, :], Y[:, cc, :], ST[0][:, :], ALU.subtract, reads=[r_y[cc], r_st[0]], writes=[fr])
                tt(f[:, :], f[:, :], ST[1][:, :], ALU.mult, reads=[fr, r_st[1]], writes=[fr])
                act(CSB[:, cc, :], f[:, :], AF.Silu, reads=[fr, r_cst], writes=[r_qcs],
                    bias=cc_(C_LB + cc), scale=cc_(C_LG + cc))
            if debug and g == 0:
                dma(SP, next_dslot(), dbg["d_cs"], CSB[:, :, :].rearrange("p a b -> p (a b)"), reads=[r_qcs])

            fence_mg = res_tokens(r_y)
            for j in range(KC):
                W, wres, _ = w_next("ga%d" % j)
                bk, br = next_bank()
                mm_chain(bk[:, :], [(wtile(W, kc), U[:, kc, 128:640]) for kc in range(KC)], reads=[wres, r_u], bres=br)
                w_done()
                fa, fra = next_ft()
                act(fa[:, :], bk[:, :], AF.Sigmoid, reads=[br, r_cst], writes=[fra], bias=cc_(C_BIN + 52 + 2 * j))
                W, wres, _ = w_next("gb%d" % j)
                bk, br = next_bank()
                mm_chain(bk[:, :], [(wtile(W, kc), U[:, kc, 128:640]) for kc in range(KC)], reads=[wres, r_u], bres=br)
                w_done()
                fb, frb = next_ft()
                act(fb[:, :], bk[:, :], AF.Sigmoid, reads=[br, r_cst], writes=[frb], bias=cc_(C_BIN + 53 + 2 * j))
                W, wres, _ = w_next("ao%d" % j)
                bk, br = next_bank()
                mm_chain(bk[:, :], [(wtile(W, oc), OT[:, oc, :]) for oc in range(16)], reads=[wres, r_ot], bres=br)
                w_done()
                tt(fa[:, :], bk[:, :], fa[:, :], ALU.mult, reads=[br, fra], writes=[fra])
                W, wres, _ = w_next("co%d" % j)
                bk, br = next_bank()
                mm_chain(bk[:, :], [(wtile(W, cc), CSB[:, cc, :]) for cc in range(NCC)], reads=[wres, r_qcs], bres=br)
                w_done()
                stt(fb[:, :], bk[:, :], cc_(C_BCO + j), fb[:, :], ALU.add, ALU.mult, reads=[br, frb, r_cst], writes=[frb])
                tt(MG[:, j, :], fa[:, :], fb[:, :], ALU.add, reads=[fra, frb], writes=[r_mg], extra=fence_mg)
            if debug and g == 0:
                dma(SP, next_dslot(), dbg["d_mg"], MG[:, :, :].rearrange("p a b -> p (a b)"), reads=[r_mg])

            fence_h1 = res_tokens([r_qcs, r_ot, r_kt, r_va] + r_cb + r_sg + r_qf + r_dg)
            pend = None
            for j in range(KC):
                W, wres, _ = w_next("wo%d" % j)
                bk, br = next_bank()
                mm_chain(bk[:, :], [(wtile(W, kc), MG[:, kc, :]) for kc in range(KC)], reads=[wres, r_mg], bres=br)
                w_done()
                xr, xres = xload(xT_d[g, :, j, 128:640], 512)
                tt(H1[:, j, :], bk[:, :], xr[:, 0:512], ALU.add, reads=[br, xres], writes=[r_h1[j]], extra=fence_h1)
                sq, sqr = next_sq()
                act(sq[:, 0:512], H1[:, j, :], AF.Square, reads=[r_h1[j]], writes=[sqr])
                if pend is not None:
                    op(PE, mm1(banks[6][:, :], ONEB[:, :], pend[0][:, 0:512], pend[2] == 0, False),
                       reads=[pend[1], r_const], writes=[r_bank[6]])
                pend = (sq, sqr, j)
            op(PE, mm1(banks[6][:, :], ONEB[:, :], pend[0][:, 0:512], False, True), reads=[pend[1], r_const], writes=[r_bank[6]])
            act(ST[0][:, :], banks[6][:, :], AF.Ln, reads=[r_bank[6], r_const], writes=[r_st[0]], bias=EPSB[:, 0:1], scale=1.0 / D)
            act(ST[0][:, :], ST[0][:, :], AF.Exp, reads=[r_st[0]], writes=[r_st[0]], scale=-0.5)
            if debug and g == 0:
                dma(SP, next_dslot(), dbg["d_h1"], H1[:, :, :].rearrange("p a b -> p (a b)"), reads=r_h1)
            for kc in range(KC):
                stt(U[:, kc, 0:512], H1[:, kc, :], cc_(C_G2 + kc), ST[0][:, :], ALU.mult, ALU.mult,
                    reads=[r_h1[kc], r_st[0], r_cst], writes=[r_u])

            fence_act = res_tokens([r_mg])
            f0 = 0
            nxt = g + 1 if g + 1 < ng else None
            sidx = 0
            for fg, nf in enumerate(FG_SIZES):
                ai = fg % 2
                last = (fg == len(FG_SIZES) - 1)
                for t in range(nf):
                    W, wres, _ = w_next("fgate%d" % (f0 + t))
                    bg, brg = next_bank()
                    mm_chain(bg[:, :], [(wtile(W, kc), U[:, kc, 0:512]) for kc in range(KC)], reads=[wres, r_u], bres=brg)
                    w_done()
                    W, wres, _ = w_next("fup%d" % (f0 + t))
                    bu, bru = next_bank()
                    mm_chain(bu[:, :], [(wtile(W, kc), U[:, kc, 0:512]) for kc in range(KC)], reads=[wres, r_u], bres=bru)
                    w_done()
                    f, fr = next_ft()
                    act(f[:, :], bg[:, :], AF.Silu, reads=[brg], writes=[fr])
                    tt(ACT[ai][:, t, :], f[:, :], bu[:, :], ALU.mult, reads=[fr, bru], writes=[r_act[ai]], extra=fence_act)
                    if nxt is not None and sidx < KC:
                        stats_chunk(nxt, sidx)
                        sidx += 1
                        if sidx == KC:
                            stats_finish_own()
                pendq = []
                for j in range(KC):
                    W, wres, _ = w_next("wd%d_%d" % (fg, j))
                    bk, br = next_bank()
                    mm_chain(bk[:, :], [(wtile(W, t), ACT[ai][:, t, :]) for t in range(nf)], reads=[wres, r_act[ai]], bres=br)
                    w_done()
                    tt(H1[:, j, :], bk[:, :], H1[:, j, :], ALU.add, reads=[br, r_h1[j]], writes=[r_h1[j]])
                    if last:
                        sq, sqr = next_sq()
                        act(sq[:, 0:512], H1[:, j, :], AF.Square, reads=[r_h1[j]], writes=[sqr])
                        pendq.append((sq, sqr, j))
                        if len(pendq) > 1:
                            p_ = pendq.pop(0)
                            op(PE, mm1(banks[6][:, :], ONEB[:, :], p_[0][:, 0:512], p_[2] == 0, False),
                               reads=[p_[1], r_const], writes=[r_bank[6]])
                        if nxt is not None:
                            u_chunk(nxt, j)
                if last:
                    while pendq:
                        p_ = pendq.pop(0)
                        op(PE, mm1(banks[6][:, :], ONEB[:, :], p_[0][:, 0:512], p_[2] == 0, len(pendq) == 0),
                           reads=[p_[1], r_const], writes=[r_bank[6]])
                f0 += nf

            act(ST[1][:, :], banks[6][:, :], AF.Ln, reads=[r_bank[6], r_const], writes=[r_st[1]], bias=EPSB[:, 0:1], scale=1.0 / D)
            act(ST[1][:, :], ST[1][:, :], AF.Exp, reads=[r_st[1]], writes=[r_st[1]], scale=-0.5)
            def fn_op(g_, j):
                def run():
                    i = state["ft"]
                    f, fr = next_ft()
                    stt(f[:, :], H1[:, j, :], cc_(C_GF + j), ST[1][:, :], ALU.mult, ALU.mult,
                        reads=[r_h1[j], r_st[1], r_cst], writes=[fr])
                    dma(SP, oslots[i], out_d[g_, j * 128:(j + 1) * 128, :], f[:, :], reads=[fr])
                    fn_done.add(j)
                return run
            fn_q.extend(fn_op(g, j) for j in list(range(28, KC)) + list(range(28)))
            fn_done.clear()
            if g == ng - 1:
                emit_fn(len(fn_q))

        final = [Tok(s.sem, 16 * s.cnt, None, "D" + s.name) for s in oslots + dslots if s.cnt > 0]
        SP.record(None, final, inc=False)

        with nc.Block() as block:
            @block.tensor
            def _(h):
                PE.replay(h)

            @block.scalar
            def _(h):
                AC.replay(h)

            @block.vector
            def _(h):
                DV.replay(h)

            @block.gpsimd
            def _(h):
                PO.replay(h)

            @block.sync
            def _(h):
                SP.replay(h)
    return nc


_NC_CACHE = {}


def kernel(x, meta_tokens, mix_norm_g, w_in, b_in, attn_sinks, conv_w, conv_b, conv_ln_g, conv_ln_b,
           w_attn_o, w_conv_o, b_conv_o, w_out, ffn_norm_g, w_gate_up, w_down, final_norm_g):
    f = lambda a: np.asarray(a, dtype=np.float32)
    x = f(x); meta = f(meta_tokens)
    wts = pack_weights(f(w_in)[0], f(w_attn_o)[0], f(w_conv_o)[0], f(w_out)[0], f(w_gate_up)[0], f(w_down)[0])
    cst = pack_consts(f(b_in)[0], f(b_conv_o)[0], f(mix_norm_g)[0], f(ffn_norm_g)[0], f(final_norm_g),
                      f(conv_w)[0], f(conv_b)[0], f(conv_ln_g)[0], f(conv_ln_b)[0], f(attn_sinks)[0])
    bv = np.ascontiguousarray(np.broadcast_to(f(b_in)[0][QD + KVD:QD + 2 * KVD][None, :], (128, 256)))
    if "nc" not in _NC_CACHE:
        _NC_CACHE["nc"] = build_nc()
    nc = _NC_CACHE["nc"]
    in_maps = []
    for core in range(NCORES):
        m = core_inputs(core, x, meta)
        m.update({"cst": cst, "bv": bv, "wts": wts})
        in_maps.append(m)
    res = run_bass_kernel_spmd(nc, in_maps, core_ids=list(range(NCORES)))
    out = np.empty((BATCH, SEQ, D), np.float32)
    for core in range(NCORES):
        b, half = core // 2, core % 2
        o = np.asarray(res.results[core]["outT"])
        for g in range(NG):
            s0 = half * 2048 + g * T
            out[b, s0:s0 + T, :] = o[g].T
    return out
```
